# Optimizing a Trainium2 kernel written in Bass

```python
import jax, jax.numpy as jnp
from jax import lax
import numpy as np

D_MODEL = 1024
BATCH = 16
SEQ = 2048
DEPTH = 1

CTX_LEN = 256
GRID_W = 64
CONV_W = 512
MLSTM_HEADS = 4
HEAD_DIM = 128
MLSTM_W = MLSTM_HEADS * HEAD_DIM
MIX_W = CONV_W + MLSTM_W
CONV_COLS = 3 * CONV_W
N_GATE_COLS = 4 * MLSTM_HEADS
IN_COLS = CONV_COLS + 4 * MLSTM_W + N_GATE_COLS
FFN_DIM = 2816
CHUNK = 128
N_MOD = 9
EPS = 1e-6

kernel_name = "hybrid_conv_mlstm_macaron_dit_layer"


def _rmsnorm(x, g):
    xf = x.astype(jnp.float32)
    y = xf * lax.rsqrt(jnp.mean(xf * xf, axis=-1, keepdims=True) + EPS)
    return y.astype(x.dtype) * g


def _modulation(cvec, w_mod, b_mod):
    m = (jax.nn.silu(cvec) @ w_mod + b_mod).reshape(cvec.shape[0], N_MOD, D_MODEL)
    return tuple(m[:, i, None, :] for i in range(N_MOD))


def _swiglu(h, w_up, w_down):
    a, b = jnp.split(h @ w_up, 2, axis=-1)
    return (jax.nn.silu(a) * b) @ w_down


def _ffn_sublayer(x, w_up, w_down, g_pre, g_post, shift, scale, gate):
    h = _rmsnorm(x, g_pre) * (1 + scale) + shift
    return x + 0.5 * gate * _rmsnorm(_swiglu(h, w_up, w_down), g_post)


def _conv3_grid(u, w):
    bn, length, ch = u.shape
    rows = length // GRID_W
    gp = jnp.pad(u.reshape(bn, rows, GRID_W, ch), ((0, 0), (0, 0), (1, 1), (0, 0)))
    out = w[0] * gp[:, :, :-2] + w[1] * gp[:, :, 1:-1] + w[2] * gp[:, :, 2:]
    return out.reshape(bn, length, ch)


def _conv3_seq(u, w):
    up = jnp.pad(u, ((0, 0), (1, 1), (0, 0)))
    return w[0] * up[:, :-2] + w[1] * up[:, 1:-1] + w[2] * up[:, 2:]


def _short_conv(p, conv_w, conv_fn):
    bg, cg, u = jnp.split(p, 3, axis=-1)
    return bg * conv_fn(cg * u, conv_w)


def _zero_state(bn):
    return (jnp.zeros((bn, MLSTM_HEADS, HEAD_DIM, HEAD_DIM), jnp.float32),
            jnp.zeros((bn, MLSTM_HEADS, HEAD_DIM), jnp.float32),
            jnp.zeros((bn, MLSTM_HEADS), jnp.float32))


def _mlstm_chunkwise(q, k, v, log_i, log_f, state, with_outputs):
    bn, nh, length, dh = q.shape
    nc = length // CHUNK

    def to_chunks(a):
        return jnp.moveaxis(a.reshape(a.shape[:2] + (nc, CHUNK) + a.shape[3:]), 2, 0)

    causal = jnp.tril(jnp.ones((CHUNK, CHUNK), dtype=bool))

    def step(carry, xs):
        c_st, n_st, m_st = carry
        qc, kc, vc, ic, fc = xs
        b = jnp.cumsum(fc, axis=-1)
        b_last = b[..., -1]
        w_end = b_last[..., None] - b + ic
        m_new = jnp.maximum(b_last + m_st, jnp.max(w_end, axis=-1))
        decay = jnp.exp(b_last + m_st - m_new)
        w = jnp.exp(w_end - m_new[..., None])
        c_new = decay[..., None, None] * c_st + jnp.einsum('bhs,bhsd,bhse->bhde', w, vc, kc)
        n_new = decay[..., None] * n_st + jnp.einsum('bhs,bhse->bhe', w, kc)
        if with_outputs:
            dmat = b[..., :, None] - b[..., None, :] + ic[..., None, :]
            dmat = jnp.where(causal, dmat, -jnp.inf)
            m_inter = b + m_st[..., None]
            m_q = jnp.maximum(m_inter, jnp.max(dmat, axis=-1))
            s = jnp.einsum('bhje,bhse->bhjs', qc, kc) * jnp.exp(dmat - m_q[..., None])
            g = jnp.exp(m_inter - m_q)
            num = g[..., None] * jnp.einsum('bhde,bhje->bhjd', c_st, qc) + jnp.einsum('bhjs,bhsd->bhjd', s, vc)
            den = g * jnp.einsum('bhe,bhje->bhj', n_st, qc) + jnp.sum(s, axis=-1)
            h = num / jnp.maximum(jnp.abs(den), jnp.exp(-m_q))[..., None]
        else:
            h = None
        return (c_new, n_new, m_new), h

    state, hs = lax.scan(step, state, tuple(to_chunks(a) for a in (q, k, v, log_i, log_f)))
    if with_outputs:
        hs = jnp.moveaxis(hs, 0, 2).reshape(bn, nh, length, dh)
    return state, hs


def _mlstm_bidir(p, b_gates, state_f, state_b, with_outputs):
    bn, length, _ = p.shape
    heads = lambda a: a.reshape(bn, length, MLSTM_HEADS, HEAD_DIM).transpose(0, 2, 1, 3).astype(jnp.float32)
    q = heads(p[..., 0:MLSTM_W])
    k = heads(p[..., MLSTM_W:2 * MLSTM_W]) * (HEAD_DIM ** -0.5)
    v = heads(p[..., 2 * MLSTM_W:3 * MLSTM_W])
    o = p[..., 3 * MLSTM_W:4 * MLSTM_W]
    gates = (p[..., 4 * MLSTM_W:] + b_gates).astype(jnp.float32)
    gates = gates.reshape(bn, length, 4, MLSTM_HEADS).transpose(2, 0, 3, 1)
    i_f, i_b = gates[0], gates[1]
    lf_f, lf_b = jax.nn.log_sigmoid(gates[2]), jax.nn.log_sigmoid(gates[3])
    flip = lambda a: jnp.flip(a, axis=2)
    st_f, h_f = _mlstm_chunkwise(q, k, v, i_f, lf_f, state_f, with_outputs)
    st_b, h_b = _mlstm_chunkwise(flip(q), flip(k), flip(v), flip(i_b), flip(lf_b), state_b, with_outputs)
    h = h_f + flip(h_b) if with_outputs else None
    return h, o, st_f, st_b


def _mlstm_out(h, o, mh_norm):
    bn, nh, length, dh = h.shape
    h = h.transpose(0, 2, 1, 3)
    h = h * lax.rsqrt(jnp.mean(h * h, axis=-1, keepdims=True) + EPS)
    h = h.reshape(bn, length, nh * dh).astype(o.dtype) * mh_norm
    return jax.nn.sigmoid(o) * h


def _mixer(h_lat, h_ctx, w_in, b_gates, conv_w, mh_norm, w_out, last):
    bn = h_lat.shape[0]
    zero = _zero_state(bn)
    if last:
        _, _, st_f, st_b = _mlstm_bidir(h_ctx @ w_in[:, CONV_COLS:], b_gates, zero, zero, False)
        y_ctx = None
    else:
        pc = h_ctx @ w_in
        hc, oc, st_f, st_b = _mlstm_bidir(pc[..., CONV_COLS:], b_gates, zero, zero, True)
        conv_c = _short_conv(pc[..., :CONV_COLS], conv_w, _conv3_seq)
        y_ctx = jnp.concatenate([conv_c, _mlstm_out(hc, oc, mh_norm)], axis=-1) @ w_out
    pl = h_lat @ w_in
    hl, ol, _, _ = _mlstm_bidir(pl[..., CONV_COLS:], b_gates, st_f, st_b, True)
    conv_l = _short_conv(pl[..., :CONV_COLS], conv_w, _conv3_grid)
    y_lat = jnp.concatenate([conv_l, _mlstm_out(hl, ol, mh_norm)], axis=-1) @ w_out
    return y_lat, y_ctx


def setup_inputs(seed: int = 0) -> dict:
    key = jax.random.key(seed)
    ks = jax.random.split(key, 20)
    d = D_MODEL
    nrm = lambda k, shape, s: jax.random.normal(k, shape, jnp.float32) * s
    forget_bias = jnp.tile(jnp.linspace(3.0, 6.0, MLSTM_HEADS, dtype=jnp.float32), 2)
    b_gates = jnp.concatenate([nrm(ks[14], (DEPTH, 2 * MLSTM_HEADS), 0.1),
                               forget_bias[None] + nrm(ks[15], (DEPTH, 2 * MLSTM_HEADS), 0.1)], axis=-1)
    return {
        "x": nrm(ks[0], (BATCH, SEQ, d), 1.0),
        "c": nrm(ks[1], (BATCH, d), 1.0),
        "ctx": nrm(ks[2], (BATCH, CTX_LEN, d), 1.0),
        "c_ctx": nrm(ks[3], (d,), 1.0),
        "w_mod": nrm(ks[4], (DEPTH, d, N_MOD * d), 0.5 * d ** -0.5),
        "b_mod": nrm(ks[5], (DEPTH, N_MOD * d), 0.02),
        "norm_g": 1.0 + nrm(ks[6], (DEPTH, 6, d), 0.05),
        "ffn1_up": nrm(ks[7], (DEPTH, d, 2 * FFN_DIM), d ** -0.5),
        "ffn1_down": nrm(ks[8], (DEPTH, FFN_DIM, d), FFN_DIM ** -0.5),
        "ffn2_up": nrm(ks[9], (DEPTH, d, 2 * FFN_DIM), d ** -0.5),
        "ffn2_down": nrm(ks[10], (DEPTH, FFN_DIM, d), FFN_DIM ** -0.5),
        "w_in": nrm(ks[11], (DEPTH, d, IN_COLS), d ** -0.5),
        "b_gates": b_gates,
        "conv_w": nrm(ks[12], (DEPTH, 3, CONV_W), 0.5),
        "mh_norm": 1.0 + nrm(ks[13], (DEPTH, MLSTM_W), 0.05),
        "w_out": nrm(ks[16], (DEPTH, MIX_W, d), MIX_W ** -0.5),
    }


def reference(x, c, ctx, c_ctx, w_mod, b_mod, norm_g, ffn1_up, ffn1_down, ffn2_up, ffn2_down,
              w_in, b_gates, conv_w, mh_norm, w_out):
    for l in range(DEPTH):
        last = l == DEPTH - 1
        g = norm_g[l]
        m_lat = _modulation(c, w_mod[l], b_mod[l])
        m_ctx = _modulation(c_ctx[None], w_mod[l], b_mod[l])
        x = _ffn_sublayer(x, ffn1_up[l], ffn1_down[l], g[0], g[1], *m_lat[0:3])
        ctx = _ffn_sublayer(ctx, ffn1_up[l], ffn1_down[l], g[0], g[1], *m_ctx[0:3])
        h_lat = _rmsnorm(x, g[2]) * (1 + m_lat[4]) + m_lat[3]
        h_ctx = _rmsnorm(ctx, g[2]) * (1 + m_ctx[4]) + m_ctx[3]
        y_lat, y_ctx = _mixer(h_lat, h_ctx, w_in[l], b_gates[l], conv_w[l], mh_norm[l], w_out[l], last)
        x = x + m_lat[5] * _rmsnorm(y_lat, g[3])
        x = _ffn_sublayer(x, ffn2_up[l], ffn2_down[l], g[4], g[5], *m_lat[6:9])
        if not last:
            ctx = ctx + m_ctx[5] * _rmsnorm(y_ctx, g[3])
            ctx = _ffn_sublayer(ctx, ffn2_up[l], ffn2_down[l], g[4], g[5], *m_ctx[6:9])
    return x
```

```python
import os
import numpy as np
import concourse.bass as bass
import concourse.mybir as mybir
from concourse.bass_utils import run_bass_kernel_spmd
from concourse.alu_op_type import AluOpType as ALU
from contextlib import ExitStack

F32 = mybir.dt.float32
BF16 = mybir.dt.bfloat16
AF = mybir.ActivationFunctionType
AX = mybir.AxisListType

ENGS = ("tensor", "vector", "scalar", "gpsimd", "sync")

D = 1024
FF = 2816
NJ = 22
L = 2048
NT = 16
CL = 256
NH = 4
HD = 128
EPS = 1e-6
NCORES = 8
TB = 4


class Sched:
    def __init__(self, nc, stack, same_engine_sync=True):
        self.nc = nc
        self.cnt = {e: 0 for e in ENGS}
        self.seen = {e: {} for e in ENGS}
        self.lastw = {}
        self.reads = {}
        self.sems = {}
        self.stack = stack
        self.same = same_engine_sync
        self.dmacnt = {}
        for e in ENGS:
            self.sems[e] = stack.enter_context(nc.semaphore("p_" + e))
        self.nwaits = 0
        self.nops = 0
        self.dead = False
        self.E = {"tensor": nc.tensor, "vector": nc.vector, "scalar": nc.scalar,
                  "gpsimd": nc.gpsimd, "sync": nc.sync}

    def _sem(self, key):
        if key not in self.sems:
            self.sems[key] = self.stack.enter_context(self.nc.semaphore("d_" + str(key)))
            self.dmacnt[key] = 0
        return self.sems[key]

    def _deps(self, eng, reads, writes):
        need = {}

        def req(tok, war=False):
            if tok is None:
                return
            k, v = tok
            if k == eng:
                if war or not self.same or v > self.cnt[eng]:
                    return
            if need.get(k, 0) < v:
                need[k] = v

        for r in reads:
            req(self.lastw.get(r))
        for w in writes:
            req(self.lastw.get(w))
            for t in self.reads.get(w, ()):
                req(t, war=True)
        for k, v in need.items():
            if self.seen[eng].get(k, 0) >= v:
                continue
            if k in self.cnt:
                assert v <= self.cnt[k], f"dep on unsignalled op: {eng} needs {k}>={v}, issued {self.cnt[k]}"
            else:
                assert v <= self.dmacnt[k], f"dep on unissued dma {k} {v}"
            self.seen[eng][k] = v
            self.E[eng].wait_ge(self.sems[k], v)
            self.nwaits += 1

    def _record(self, tok, reads, writes):
        for r in reads:
            self.reads.setdefault(r, []).append(tok)
        for w in writes:
            self.lastw[w] = tok
            self.reads[w] = []

    def op(self, eng, fn, reads=(), writes=(), inc=True):
        if self.dead:
            return
        self._deps(eng, reads, writes)
        tok = (eng, self.cnt[eng] + 1)
        if inc:
            self.cnt[eng] += 1
        ins = fn(self.E[eng])
        if inc:
            ins.then_inc(self.sems[eng], 1)
        self.nops += 1
        self._record(tok, reads, writes)

    def dma(self, eng, semkey, items):
        if self.dead:
            return None
        self._sem(semkey)
        allr = [r for it in items for r in it[2]]
        allw = [w for it in items for w in it[3]]
        self._deps(eng, allr, allw)
        self.dmacnt[semkey] += 16 * len(items)
        tok = (semkey, self.dmacnt[semkey])
        for (o, i, r, w, kw) in items:
            self.E[eng].dma_start(out=o, in_=i, **kw).then_inc(self.sems[semkey], 16)
        self._record(tok, allr, allw)
        return tok

    def wait_all(self, eng, exclude=()):
        if self.dead:
            return
        for k in list(self.cnt) + list(self.dmacnt):
            if k in exclude:
                continue
            v = self.cnt[k] if k in self.cnt else self.dmacnt[k]
            if k == eng or v == 0:
                continue
            if self.seen[eng].get(k, 0) >= v:
                continue
            self.seen[eng][k] = v
            self.E[eng].wait_ge(self.sems[k], v)

    def barrier(self, exclude=()):
        for e in ENGS:
            self.wait_all(e, exclude)


def _consts_np():
    c = np.zeros((128, 1024), np.float32)
    i = np.arange(128)
    c[:, 0:128] = np.eye(128)
    c[:, 128:256] = (i[:, None] <= i[None, :])
    c[:, 256:384] = (i[:, None] >= i[None, :])
    c[:, 384:512] = 1.0
    for r in range(3):
        c[r, 512 + r * 128: 512 + (r + 1) * 128] = 1.0
    c[0:4, 896:900] = np.eye(4)
    c[0:3, 904:907] = np.eye(3)
    c[0:4, 908] = 1.0
    c[4:8, 908] = -1.0
    c[4:8, 909] = 1.0
    return c


class _Stop(Exception):
    pass


def build_program(dbg=False, stop=None):
    nc = bass.Bass("TRN2", target_bir_lowering=False)

    sref = []

    def ckpt(name):
        if stop == name and not sref[0].dead:
            sref[0].barrier()
            sref[0].dead = True

    def din(name, shape, dt=F32):
        return nc.dram_tensor(name, list(shape), dt, kind="ExternalInput").ap()

    x_d = din("x", [2, L, D])
    ctx_d = din("ctx", [2 * CL, D])
    cc_d = din("cc", [3, D])
    wmod_d = din("w_mod", [D, 9 * D])
    bmod_d = din("b_mod3", [3, 9 * D])
    ng_d = din("norm_g3", [3, 6 * D])
    w1u_d = din("w1u", [D, 2 * FF])
    w1d_d = din("w1d", [FF, D])
    w2u_d = din("w2u", [D, 2 * FF])
    w2d_d = din("w2d", [FF, D])
    win_d = din("w_in_main", [D, 3584])
    wing_d = din("w_in_g", [D, 16])
    bg_d = din("bgT", [4, 4])
    bg8_d = din("bg8", [8, 2])
    cw_d = din("conv_wT", [128, 12])
    mhn_d = din("mhn_b", [128, 512])
    wout_d = din("w_out", [D, D])
    const_d = din("consts", [128, 1024])
    out_d = nc.dram_tensor("out", [2, L, D], F32, kind="ExternalOutput").ap()

    def dscr(name, shape, dt):
        return nc.dram_tensor(name, list(shape), dt, kind="Internal").ap()

    w1u_b = dscr("w1u_b", [D, 2 * FF], BF16)
    w1d_b = dscr("w1d_b", [FF, D], BF16)
    w2u_b = dscr("w2u_b", [D, 2 * FF], BF16)
    w2d_b = dscr("w2d_b", [FF, D], BF16)
    win_b = dscr("win_b", [D, 3584], BF16)
    wout_b = dscr("wout_b", [D, D], BF16)
    grow_d = dscr("grow_d", [3, 3 * D], F32)

    dbg_out = {}
    if dbg:
        dbg_out["dbg_x1"] = nc.dram_tensor("dbg_x1", [L, D], F32, kind="ExternalOutput").ap()
        dbg_out["dbg_mix"] = nc.dram_tensor("dbg_mix", [128, 8 * L], BF16, kind="ExternalOutput").ap()
        dbg_out["dbg_x2"] = nc.dram_tensor("dbg_x2", [L, D], F32, kind="ExternalOutput").ap()

    with ExitStack() as st:
        s = Sched(nc, st)
        sref.append(s)

        tcount = [0]

        def T(name, shape, dt, stack=None):
            tcount[0] += 1
            return (stack or st).enter_context(nc.sbuf_tensor(f"s{tcount[0]}_{name}", list(shape), dt))

        PT = [st.enter_context(nc.psum_tensor(f"PT{i}", [128, 1024], BF16)) for i in range(2)]
        PF = [st.enter_context(nc.psum_tensor(f"PF{i}", [128, 512], F32)) for i in range(6)]

        cst = T("cst", [128, 1024], F32)
        identb = T("identb", [128, 128], BF16)
        maskb = T("maskb", [128, 2, 128], F32)
        modT = T("modT", [128, 144], F32)
        gtile = T("gtile", [128, D], F32)
        mhalf = T("mhalf", [128, 8], F32)
        wing = T("wing", [128, 8, 16], BF16)
        bgT = T("bgT", [4, 4], F32)
        mdump = T("mdump", [4, 2], F32)
        cwT = T("cwT", [128, 12], F32)
        mhnb = T("mhnb", [128, 512], F32)
        Cst = T("Cst", [128, 2, 2, 512], F32)
        nst = T("nst", [128, 2, 2, 4], F32)
        mst = T("mst", [4, 2, 2], F32)
        stat = T("stat", [128, 256], F32)
        junk = [T(f"junk{i}", [128, D], BF16) for i in range(1)]

        statn = [0]

        def scol(n=1):
            if statn[0] + n > 256:
                statn[0] = 0
            c0 = statn[0]
            statn[0] += n
            return stat[:, c0:c0 + n], ("stat", c0)

        ident_f = cst[:, 0:128]
        ones_f = cst[:, 384:512]

        s.dma("sync", "c0", [(cst[:], const_d, [], ["cst"], {})])
        s.dma("sync", "c1", [(bgT[:], bg_d, [], ["bgT"], {}), (cwT[:], cw_d, [], ["cwT"], {}),
                             (mhnb[:], mhn_d, [], ["mhnb"], {})])
        s.op("vector", lambda e: e.tensor_copy(out=identb[:], in_=cst[:, 0:128]), ["cst"], ["identb"])
        s.op("vector", lambda e: e.tensor_copy(out=maskb[:].rearrange("p a b -> p (a b)"), in_=cst[:, 128:384]), ["cst"], ["maskb"])
        s.op("vector", lambda e: e.memset(mhalf[:], -0.5), [], ["mhalf"])
        s.op("vector", lambda e: e.tensor_scalar(out=mhnb[:], in0=mhnb[:], scalar1=0.5, scalar2=None, op0=ALU.mult), ["mhnb"], ["mhnb"])
        s.dma("gpsimd", "cv1", [(w1u_b, w1u_d, [], ["w1u_b"], {})])
        s.dma("gpsimd", "cv2", [(w1d_b, w1d_d, [], ["w1d_b"], {})])
        wingv = wing_d.rearrange("(kc p) n -> p kc n", p=128)
        s.dma("gpsimd", "cv3", [(wing[:], wingv, [], ["wing"], {})])

        with ExitStack() as sp:
            cc = T("cc", [3, D], F32, sp)
            scc = T("scc", [3, D], F32, sp)
            sccT = T("sccT", [128, 8, 3], BF16, sp)
            mrow = T("mrow", [3, 9 * D], F32, sp)
            ng3 = T("ng3", [3, 6 * D], F32, sp)
            grow = T("grow", [3, 3 * D], F32, sp)
            arow = T("arow", [3, 3 * D], F32, sp)
            wmt = [T(f"wmt{i}", [128, 8, 512], BF16, sp) for i in range(3)]
            s.dma("sync", "c2", [(cc[:], cc_d, [], ["cc"], {}), (mrow[:], bmod_d, [], ["mrow"], {}),
                                 (ng3[:], ng_d, [], ["ng3"], {})])
            s.op("scalar", lambda e: e.activation(out=scc[:], in_=cc[:], func=AF.Silu), ["cc"], ["scc"])
            for kc in range(8):
                s.op("tensor", lambda e, kc=kc: e.matmul(PF[0][:, kc * 3:(kc + 1) * 3], lhsT=scc[0:3, kc * 128:(kc + 1) * 128],
                                                          rhs=cst[0:3, 904:907], start=True, stop=True),
                     ["scc", "cst"], ["PF0"], inc=(kc == 7))
            s.op("vector", lambda e: e.tensor_copy(out=sccT[:].rearrange("p a b -> p (a b)"), in_=PF[0][:, 0:24]), ["PF0"], ["sccT"])
            wmv = wmod_d.rearrange("(kc p) n -> p kc n", p=128)
            for n in range(18):
                sl = wmt[n % 3]
                key = ("wmt", n % 3)
                s.dma("gpsimd", f"wm{n % 3}", [(sl[:], wmv[:, :, n * 512:(n + 1) * 512], [], [key], {})])
                pb = PF[1 + (n % 2)]
                pk = f"PF{1 + (n % 2)}"
                for kc in range(8):
                    s.op("tensor", lambda e, kc=kc, sl=sl, pb=pb: e.matmul(pb[0:3, :], lhsT=sccT[:, kc, :], rhs=sl[:, kc, :],
                                                                            start=(kc == 0), stop=(kc == 7)),
                         ["sccT", key], [pk], inc=(kc == 7))
                s.op("vector", lambda e, n=n, pb=pb: e.tensor_tensor(out=mrow[:, n * 512:(n + 1) * 512], in0=pb[0:3, :],
                                                                      in1=mrow[:, n * 512:(n + 1) * 512], op=ALU.add),
                     [pk, "mrow"], ["mrow"])
            s.dma("gpsimd", "cv4", [(win_b, win_d, [], ["win_b"], {})])
            s.dma("gpsimd", "cv5", [(wout_b, wout_d, [], ["wout_b"], {})])
            s.dma("gpsimd", "cv6", [(w2u_b, w2u_d, [], ["w2u_b"], {})])
            s.dma("gpsimd", "cv7", [(w2d_b, w2d_d, [], ["w2d_b"], {})])
            for i in range(3):
                coef = 1.0 if i == 1 else 0.5
                s.op("vector", lambda e, i=i: e.scalar_tensor_tensor(out=arow[:, i * D:(i + 1) * D], in0=mrow[:, (3 * i + 1) * D:(3 * i + 2) * D],
                                                                      scalar=1.0, in1=ng3[:, (2 * i) * D:(2 * i + 1) * D], op0=ALU.add, op1=ALU.mult),
                     ["mrow", "ng3"], ["arow"])
                s.op("vector", lambda e, i=i, coef=coef: e.scalar_tensor_tensor(out=grow[:, i * D:(i + 1) * D], in0=mrow[:, (3 * i + 2) * D:(3 * i + 3) * D],
                                                                                  scalar=coef, in1=ng3[:, (2 * i + 1) * D:(2 * i + 2) * D], op0=ALU.mult, op1=ALU.mult),
                     ["mrow", "ng3"], ["grow"])
            s.dma("sync", "c3", [(grow_d, grow[:], ["grow"], ["grow_d"], {})])
            for i in range(3):
                for a_s in range(2):
                    for kc in range(8):
                        col = ((i * 2 + a_s) * 8 + kc) * 3
                        src = arow[0:3, i * D + kc * 128: i * D + (kc + 1) * 128] if a_s == 0 else mrow[0:3, 3 * i * D + kc * 128: 3 * i * D + (kc + 1) * 128]
                        last = (i == 2 and a_s == 1 and kc == 7)
                        s.op("tensor", lambda e, col=col, src=src: e.matmul(PF[3][:, col:col + 3], lhsT=src, rhs=cst[0:3, 904:907], start=True, stop=True),
                             ["arow", "mrow", "cst"], ["PF3"], inc=last)
            s.op("vector", lambda e: e.tensor_copy(out=modT[:], in_=PF[3][:, 0:144]), ["PF3"], ["modT"])
            s.barrier(exclude=("cv4", "cv5", "cv6", "cv7"))

        def mod_AS(i, kc, r):
            a = ((i * 2 + 0) * 8 + kc) * 3 + r
            b = ((i * 2 + 1) * 8 + kc) * 3 + r
            return modT[:, a:a + 1], modT[:, b:b + 1]

        def load_gate(i, r):
            src = grow_d[r:r + 1, i * D:(i + 1) * D].partition_broadcast(128)
            s.dma("sync", "gt", [(gtile[:], src, ["grow_d"], ["gtile"], {})])

        rr = {"pt": 0, "xn": 0, "jk": 0}

        def prenorm_a(grp, xnb):
            st_ = []
            for (xap, xkey, col0) in grp:
                ss, ssk = scol(3)
                xn = xnb[rr["xn"] % 2]
                xnk = ("xn", rr["xn"] % 2)
                rr["xn"] += 1
                pt = PT[rr["pt"] % 2]
                ptk = f"PT{rr['pt'] % 2}"
                rr["pt"] += 1
                st_.append((xap, xkey, col0, ss, ssk, xn, xnk, pt, ptk))
            for (xap, xkey, col0, ss, ssk, xn, xnk, pt, ptk) in st_:
                jk = junk[0]
                jkk = ("junk", 0)
                s.op("scalar", lambda e, jk=jk, xap=xap, ss=ss: e.activation(out=jk[:], in_=xap, func=AF.Square, accum_out=ss[:, 0:1]), [xkey], [ssk, jkk])
            for (xap, xkey, col0, ss, ssk, xn, xnk, pt, ptk) in st_:
                s.op("vector", lambda e, ss=ss: e.tensor_scalar(out=ss[:, 1:2], in0=ss[:, 0:1], scalar1=1.0 / D, scalar2=EPS, op0=ALU.mult, op1=ALU.add), [ssk], [ssk])
            for (xap, xkey, col0, ss, ssk, xn, xnk, pt, ptk) in st_:
                s.op("gpsimd", lambda e, ss=ss: e.tensor_tensor(out=ss[:, 2:3], in0=ss[:, 1:2], in1=mhalf[:, 0:1], op=ALU.pow), [ssk, "mhalf"], [ssk])
            for (xap, xkey, col0, ss, ssk, xn, xnk, pt, ptk) in st_:
                s.op("vector", lambda e, ss=ss, xn=xn, xap=xap: e.tensor_scalar(out=xn[:], in0=xap, scalar1=ss[:, 2:3], scalar2=None, op0=ALU.mult), [xkey, ssk], [xnk])
            return st_

        def prenorm_b(st_, hT, hkey, i, r):
            for (xap, xkey, col0, ss, ssk, xn, xnk, pt, ptk) in st_:
                for kc in range(8):
                    s.op("tensor", lambda e, kc=kc, pt=pt, xn=xn: e.transpose(out=pt[:, kc * 128:(kc + 1) * 128], in_=xn[:, kc * 128:(kc + 1) * 128], identity=identb[:]),
                         [xnk, "identb"], [ptk], inc=(kc == 7))
            for (xap, xkey, col0, ss, ssk, xn, xnk, pt, ptk) in st_:
                for kc in range(8):
                    A, S = mod_AS(i, kc, r)
                    s.op("vector", lambda e, kc=kc, A=A, S=S, pt=pt, col0=col0: e.tensor_scalar(out=hT[:, kc, col0:col0 + 128], in0=pt[:, kc * 128:(kc + 1) * 128],
                                                                                           scalar1=A, scalar2=S, op0=ALU.mult, op1=ALU.add),
                         [ptk, "modT"], [hkey])

        def prenorm_tiles(tiles, hT, hkey, i, r, xnb):
            for p0 in range(0, len(tiles), 2):
                prenorm_b(prenorm_a(tiles[p0:p0 + 2], xnb), hT, hkey, i, r)

        def prenorm_hooks(tiles, hT, hkey, i, r, xnb, steps):
            box = {}

            def a0():
                box[0] = prenorm_a(tiles[0:2], xnb)

            def b0():
                prenorm_b(box[0], hT, hkey, i, r)

            def a1():
                box[1] = prenorm_a(tiles[2:4], xnb)

            def b1():
                prenorm_b(box[1], hT, hkey, i, r)
            return dict(zip(steps, (a0, b0, a1, b1)))

        def ffn_block(xtiles, r, i, wu_b, wukey, wd, wdkey, hT, hkey, uT, wus, xnb, sab, tmpb, store=None, pre_done=False, hooks=None):
            nt = len(xtiles)
            ntok = nt * 128
            if not pre_done:
                prenorm_tiles([(xap, xkey, t * 128) for t, (xap, xkey) in enumerate(xtiles)], hT, hkey, i, r, xnb)
            wuv = wu_b.rearrange("(kc p) n -> p kc n", p=128)
            for jj in range(11):
                if hooks and jj in hooks:
                    hooks[jj]()
                sl = wus[jj % len(wus)]
                slk = ("wus", jj % len(wus))
                s.dma("sync", f"wu{jj % len(wus)}", [(sl[:], wuv[:, :, jj * 512:(jj + 1) * 512], [wukey], [slk], {})])
                for jl in range(2):
                    j = jj * 2 + jl
                    pa, pak = PF[(j % 2) * 2], f"PF{(j % 2) * 2}"
                    pb, pbk = PF[(j % 2) * 2 + 1], f"PF{(j % 2) * 2 + 1}"
                    for kc in range(8):
                        s.op("tensor", lambda e, kc=kc, sl=sl, jl=jl, pa=pa: e.matmul(pa[:, 0:ntok], lhsT=sl[:, kc, jl * 256:jl * 256 + 128], rhs=hT[:, kc, 0:ntok],
                                                                                       start=(kc == 0), stop=(kc == 7)),
                             [slk, hkey], [pak], inc=(kc == 7))
                    for kc in range(8):
                        s.op("tensor", lambda e, kc=kc, sl=sl, jl=jl, pb=pb: e.matmul(pb[:, 0:ntok], lhsT=sl[:, kc, jl * 256 + 128:jl * 256 + 256], rhs=hT[:, kc, 0:ntok],
                                                                                       start=(kc == 0), stop=(kc == 7)),
                             [slk, hkey], [pbk], inc=(kc == 7))
                    sa = sab[j % 2]
                    sak = ("sa", j % 2)
                    s.op("scalar", lambda e, sa=sa, pa=pa: e.activation(out=sa[:, 0:ntok], in_=pa[:, 0:ntok], func=AF.Silu), [pak], [sak])
                    s.op("vector", lambda e, sa=sa, pb=pb, j=j: e.tensor_tensor(out=uT[:, j, 0:ntok], in0=pb[:, 0:ntok], in1=sa[:, 0:ntok], op=ALU.mult),
                         [pbk, sak], [("uT", j)])
            for t, (xap, xkey) in enumerate(xtiles):
                p0 = (t % 3) * 2
                py = [PF[p0], PF[p0 + 1]]
                pyk = [f"PF{p0}", f"PF{p0 + 1}"]
                for hf in range(2):
                    for kc in range(NJ):
                        s.op("tensor", lambda e, kc=kc, hf=hf, t=t: e.matmul(py[hf][:, :], lhsT=uT[:, kc, t * 128:(t + 1) * 128], rhs=wd[:, kc, hf * 512:(hf + 1) * 512],
                                                                              start=(kc == 0), stop=(kc == NJ - 1)),
                             [("uT", kc), wdkey], [pyk[hf]], inc=(kc == NJ - 1))
                postnorm_residual(py, pyk, xap, xkey, tmpb)
                if store is not None:
                    s.dma("sync", "st", [(store[t], xap, [xkey], [("out", id(store), t)], {})])

        def postnorm_residual(py, pyk, xap, xkey, tmpb):
            ss, ssk = scol(4)
            jk = junk[0]
            jkk = ("junk", 0)
            rr["jk"] += 1
            s.op("scalar", lambda e: e.activation(out=jk[:, 0:512], in_=py[0][:, :], func=AF.Square, accum_out=ss[:, 0:1]), [pyk[0]], [ssk, jkk])
            s.op("scalar", lambda e: e.activation(out=jk[:, 512:1024], in_=py[1][:, :], func=AF.Square, accum_out=ss[:, 1:2]), [pyk[1]], [ssk, jkk])
            s.op("vector", lambda e: e.tensor_tensor(out=ss[:, 2:3], in0=ss[:, 0:1], in1=ss[:, 1:2], op=ALU.add), [ssk], [ssk])
            s.op("vector", lambda e: e.tensor_scalar(out=ss[:, 2:3], in0=ss[:, 2:3], scalar1=1.0 / D, scalar2=EPS, op0=ALU.mult, op1=ALU.add), [ssk], [ssk])
            s.op("gpsimd", lambda e: e.tensor_tensor(out=ss[:, 3:4], in0=ss[:, 2:3], in1=mhalf[:, 0:1], op=ALU.pow), [ssk, "mhalf"], [ssk])
            tmp = tmpb[rr["xn"] % 2]
            tk = ("tmpb", rr["xn"] % 2)
            rr["xn"] += 1
            for hf in range(2):
                s.op("vector", lambda e, hf=hf: e.scalar_tensor_tensor(out=tmp[:, hf * 512:(hf + 1) * 512], in0=py[hf][:, :], scalar=ss[:, 3:4],
                                                                        in1=gtile[:, hf * 512:(hf + 1) * 512], op0=ALU.mult, op1=ALU.mult),
                     [pyk[hf], ssk, "gtile"], [tk])
            s.op("gpsimd", lambda e: e.tensor_tensor(out=xap, in0=xap, in1=tmp[:], op=ALU.add), [xkey, tk], [xkey])

        def gates_block8(hT, hkey, ntok, chunk0, gt8):
            nck = ntok // 128
            zi, l1, cf, av = gt8
            gk = [("gt8", i) for i in range(4)]
            for gi, c0 in enumerate((0, 8)):
                pg, pgk = PF[4 + gi], f"PF{4 + gi}"
                for kc in range(8):
                    s.op("tensor", lambda e, kc=kc, c0=c0, pg=pg: e.matmul(pg[0:8, 0:ntok], lhsT=wing[:, kc, c0:c0 + 8], rhs=hT[:, kc, 0:ntok], start=(kc == 0), stop=(kc == 7)),
                         ["wing", hkey], [pgk], inc=(kc == 7))
            s.op("vector", lambda e: e.tensor_scalar(out=zi[:, 0:ntok], in0=PF[4][0:8, 0:ntok], scalar1=bg8[:, 0:1], scalar2=None, op0=ALU.add), ["PF4", "bg8"], [gk[0]])
            s.op("vector", lambda e: e.tensor_scalar(out=l1[:, 0:ntok], in0=PF[5][0:8, 0:ntok], scalar1=bg8[:, 1:2], scalar2=-1.0, op0=ALU.add, op1=ALU.mult), ["PF5", "bg8"], [gk[1]])
            s.op("scalar", lambda e: e.activation(out=l1[:, 0:ntok], in_=l1[:, 0:ntok], func=AF.Exp), [gk[1]], [gk[1]])
            s.op("scalar", lambda e: e.activation(out=l1[:, 0:ntok], in_=l1[:, 0:ntok], func=AF.Ln, bias=1.0), [gk[1]], [gk[1]])
            s.op("vector", lambda e: e.tensor_tensor_scan(out=cf[:, 0:ntok], data0=restart[:, 0:ntok], data1=l1[:, 0:ntok], initial=0.0, op0=ALU.mult, op1=ALU.add),
                 [gk[1], "restart"], [gk[2]])
            cf3 = cf[:, 0:ntok].rearrange("p (c t) -> p c t", t=128)
            l13 = l1[:, 0:ntok].rearrange("p (c t) -> p c t", t=128)
            av3 = av[:, 0:ntok].rearrange("p (c t) -> p c t", t=128)
            s.op("vector", lambda e: e.tensor_copy(out=btot8[:, chunk0:chunk0 + nck], in_=cf3[:, :, 127]), [gk[2]], ["btot8"])
            s.op("vector", lambda e: e.tensor_tensor(out=av3, in0=l13, in1=btot8[:, chunk0:chunk0 + nck].unsqueeze(2).to_broadcast([8, nck, 128]), op=ALU.add),
                 [gk[1], "btot8"], [gk[3]])
            s.op("vector", lambda e: e.tensor_scalar(out=av[:, 0:ntok], in0=av[:, 0:ntok], scalar1=cst[0:8, 909:910], scalar2=None, op0=ALU.mult), [gk[3], "cst"], [gk[3]])
            s.op("vector", lambda e: e.scalar_tensor_tensor(out=cf[:, 0:ntok], in0=cf[:, 0:ntok], scalar=cst[0:8, 908:909], in1=av[:, 0:ntok], op0=ALU.mult, op1=ALU.add),
                 [gk[2], gk[3], "cst"], [gk[2]])
            s.op("vector", lambda e: e.tensor_tensor(out=av[:, 0:ntok], in0=zi[:, 0:ntok], in1=cf[:, 0:ntok], op=ALU.add), [gk[0], gk[2]], [gk[3]])
            s.op("vector", lambda e: e.tensor_reduce(out=amax8[:, chunk0:chunk0 + nck], in_=av3, axis=AX.X, op=ALU.max), [gk[3]], ["amax8"])
            for c in range(nck):
                for qi, (src_, skey) in enumerate(((av, gk[3]), (cf, gk[2]))):
                    col = ((c * 2) + qi) * 8
                    s.op("tensor", lambda e, c=c, src_=src_, col=col: e.matmul(PF[3][:, col:col + 8], lhsT=src_[0:8, c * 128:(c + 1) * 128], rhs=cst[0:8, 0:8], start=True, stop=True),
                         [skey, "cst"], ["PF3"], inc=(c == nck - 1 and qi == 1))
            pv = PF[3][:, 0:nck * 16].rearrange("p (c q x) -> p c q x", q=2, x=8)
            s.op("vector", lambda e: e.tensor_copy(out=atm[:, chunk0:chunk0 + nck, :, :].rearrange("p c d h -> p c (d h)"), in_=pv[:, :, 0, :]), ["PF3"], ["atm"])
            s.op("scalar", lambda e: e.copy(out=btm[:, chunk0:chunk0 + nck, :, :].rearrange("p c d h -> p c (d h)"), in_=pv[:, :, 1, :]), ["PF3"], ["btm"])

        def gates_split():
            s.op("vector", lambda e: e.tensor_copy(out=amax[:, 0, :], in_=amax8[0:4, :]), ["amax8"], ["amax"])
            s.op("vector", lambda e: e.tensor_copy(out=btot[:, 0, :], in_=btot8[0:4, :]), ["btot8"], ["btot"])
            s.dma("sync", "gsp", [(amax[:, 1, :], amax8[4:8, :], ["amax8"], ["amax"], {}), (btot[:, 1, :], btot8[4:8, :], ["btot8"], ["btot"], {})])

        def gates_finish(nh, nck, order, atm, btm, amax, btot, m0, mout, Mrow, grow_, wk, el, gB, tmp4):
            for d in range(2):
                prev = m0[0:nh, d:d + 1]
                pk = "m0"
                for idx, c in enumerate(order[d]):
                    s.op("vector", lambda e, c=c, d=d, prev=prev: e.tensor_tensor(out=Mrow[0:nh, d, c:c + 1], in0=prev, in1=amax[0:nh, d, c:c + 1], op=ALU.max),
                         ["amax", "m0", "mout", "mcur", "mnext"], ["Mrow"])
                    s.op("vector", lambda e, c=c, d=d, prev=prev: e.tensor_tensor(out=grow_[0:nh, d, c:c + 1], in0=prev, in1=Mrow[0:nh, d, c:c + 1], op=ALU.subtract),
                         ["Mrow", "m0", "mout", "mcur", "mnext"], ["growg"])
                    s.op("vector", lambda e, c=c, d=d: e.tensor_tensor(out=tmp4[0:nh, d, c:c + 1], in0=Mrow[0:nh, d, c:c + 1], in1=btot[0:nh, d, c:c + 1], op=ALU.subtract),
                         ["Mrow", "btot"], ["mnext"])
                    prev = tmp4[0:nh, d, c:c + 1]
                    pk = "mnext"
                s.op("vector", lambda e, d=d, prev=prev: e.tensor_copy(out=mout[0:nh, d:d + 1], in_=prev), ["mnext"], ["mout" if mout is not mdump else "mdump"])
            s.op("scalar", lambda e: e.activation(out=grow_[0:nh, :, 0:nck], in_=grow_[0:nh, :, 0:nck], func=AF.Exp), ["growg"], ["growg"])
            for qi, (row, rk) in enumerate(((Mrow, "Mrow"), (grow_, "growg"))):
                ex, exk = tmp4, "tmp4x"
                exv = el
                R = wk
                s.op("vector", lambda e, row=row: e.tensor_tensor(out=Rexp[0:nh, :, 0:nck, :], in0=row[0:nh, :, 0:nck].unsqueeze(3).to_broadcast([nh, 2, nck, 4]),
                                                                  in1=cst[0:nh, 896:900].unsqueeze(1).unsqueeze(1).to_broadcast([nh, 2, nck, 4]), op=ALU.mult),
                     [rk, "cst"], ["Rexp"])
                for d in range(2):
                    s.op("tensor", lambda e, d=d, qi=qi: e.matmul(PF[3][:, (qi * 2 + d) * 64:(qi * 2 + d) * 64 + nck * 4], lhsT=cst[0:nh, 384:512],
                                                                    rhs=Rexp[0:nh, d, 0:nck, :].rearrange("p c h -> p (c h)"), start=True, stop=True),
                         ["Rexp", "cst"], ["PF3"], inc=True)
            Mb = PF[3][:, 0:128].rearrange("p (d c h) -> p c d h", d=2, c=16, h=4)
            gb = PF[3][:, 128:256].rearrange("p (d c h) -> p c d h", d=2, c=16, h=4)
            s.op("vector", lambda e: e.tensor_copy(out=gB[:, 0:nck, :, :], in_=gb[:, 0:nck, :, :]), ["PF3"], ["gB"])
            s.op("vector", lambda e: e.tensor_tensor(out=wk[:, 0:nck, :, :], in0=atm[:, 0:nck, :, :], in1=Mb[:, 0:nck, :, :], op=ALU.subtract), ["PF3", "atm"], ["wk"])
            s.op("vector", lambda e: e.tensor_tensor(out=el[:, 0:nck, :, :], in0=btm[:, 0:nck, :, :], in1=Mb[:, 0:nck, :, :], op=ALU.subtract), ["PF3", "btm"], ["el"])
            s.op("scalar", lambda e: e.activation(out=wk[:, 0:nck, :, :], in_=wk[:, 0:nck, :, :], func=AF.Exp), ["wk"], ["wk"])
            s.op("scalar", lambda e: e.activation(out=el[:, 0:nck, :, :], in_=el[:, 0:nck, :, :], func=AF.Exp), ["el"], ["el"])

        Rexp = T("Rexp", [4, 2, 16, 4], F32)
        restart = T("restart", [8, 512], F32)
        s.op("vector", lambda e: e.memset(restart[:], 1.0), [], ["restart"])
        s.op("vector", lambda e: e.memset(restart[:].rearrange("p (c t) -> p c t", t=128)[:, :, 0:1], 0.0), ["restart"], ["restart"])
        atm = T("atm", [128, 16, 2, 4], F32)
        btm = T("btm", [128, 16, 2, 4], F32)
        wk = T("wk", [128, 16, 2, 4], F32)
        el = T("el", [128, 16, 2, 4], F32)
        gB = T("gB", [128, 16, 2, 4], F32)
        amax = T("amax", [4, 2, 16], F32)
        btot = T("btot", [4, 2, 16], F32)
        Mrow = T("Mrow", [4, 2, 16], F32)
        growg = T("growg", [4, 2, 16], F32)
        tmp4 = T("tmp4", [4, 2, 16], F32)
        mzero = T("mzero", [4, 2], F32)
        bg8 = T("bg8", [8, 2], F32)
        amax8 = T("amax8", [8, 16], F32)
        btot8 = T("btot8", [8, 16], F32)
        s.dma("sync", "c4", [(bg8[:], bg8_d, [], ["bg8"], {})])
        s.op("vector", lambda e: e.memset(amax8[:], 0.0), [], ["amax8"])
        s.op("vector", lambda e: e.memset(btot8[:], 0.0), [], ["btot8"])
        mcur = T("mcur", [4, 2], F32)
        s.op("vector", lambda e: e.memset(mzero[:], 0.0), [], ["m0"])

        def scan_gen(d, chunks, nh, h0, kT, ktk, vx, vxk, qT, qtk, C, Ck, nS, nk, sc, outputs, out_cb=None):
            W = nh * 128
            pt = PT[d]
            ps, pn, pu = PF[d], PF[2 + d], PF[4 + d]
            kS, kN, kD, kU, kPk = ("ps", d), ("pn", d), ("pn", d), ("pu", d), ("ptK", d)
            pun, kUn = (pu, kU) if nh == 2 else (pn, kN)
            ktm, ktmk = sc["ktm"], ("ktm", d)
            vw, vwk = sc["vw"], ("vw", d)
            C3 = C.rearrange("p (h x) -> p h x", h=nh)
            for (ci, gi) in chunks:
                tsl = slice(ci * 128, (ci + 1) * 128)
                for h in range(nh):
                    s.op("tensor", lambda e, h=h: e.transpose(out=pt[:, h * 128:(h + 1) * 128], in_=kT[:, h, tsl], identity=identb[:]),
                         [ktk, "identb"], [kPk], inc=(h == nh - 1))
                    yield
                s.op("scalar", lambda e: e.copy(out=ktm[:, 0:W], in_=pt[:, 0:W]), [kPk], [ktmk])
                yield
                s.op("gpsimd", lambda e, ci=ci, gi=gi: e.tensor_tensor(out=vw[:, 0:nh, :], in0=vx[:, ci, 0:nh, :],
                                                                        in1=wk[:, gi, d, h0:h0 + nh].unsqueeze(2).to_broadcast([128, nh, 129]), op=ALU.mult),
                     [vxk, "wk"], [vwk])
                yield
                s.op("vector", lambda e, gi=gi: e.tensor_tensor(out=C3, in0=C3, in1=gB[:, gi, d, h0:h0 + nh].unsqueeze(2).to_broadcast([128, nh, 128]), op=ALU.mult),
                     [Ck, "gB"], [Ck])
                yield
                s.op("vector", lambda e, gi=gi: e.tensor_tensor(out=nS, in0=nS, in1=gB[:, gi, d, h0:h0 + nh], op=ALU.mult), [nk, "gB"], [nk])
                yield
                if outputs:
                    cgb, cgk = sc["cgb"], ("cgb", d)
                    ngb = sc["ngb"]
                    sm, smk = sc["sm"], ("sm", d)
                    s.op("scalar", lambda e: e.copy(out=cgb[:, 0:W], in_=C), [Ck], [cgk])
                    yield
                    s.op("scalar", lambda e: e.copy(out=ngb[:, 0:nh], in_=nS), [nk], [cgk])
                    yield
                    for h in range(nh):
                        s.op("tensor", lambda e, h=h: e.matmul(ps[:, h * 128:(h + 1) * 128], lhsT=kT[:, h, tsl], rhs=qT[:, h, tsl], start=True, stop=True),
                             [ktk, qtk], [kS], inc=(h == nh - 1))
                        yield
                    s.op("vector", lambda e: e.tensor_tensor(out=sm[:, 0:nh, :], in0=ps[:, 0:W].rearrange("p (h j) -> p h j", h=nh),
                                                             in1=maskb[:, d, :].unsqueeze(1).to_broadcast([128, nh, 128]), op=ALU.mult),
                         [kS, "maskb"], [smk])
                    yield
                for h in range(nh):
                    s.op("tensor", lambda e, h=h: e.matmul(pu[:, h * 128:(h + 1) * 128], lhsT=ktm[:, h * 128:(h + 1) * 128], rhs=vw[:, h, 0:128], start=True, stop=True),
                         [ktmk, vwk], [kU], inc=(h == nh - 1))
                    yield
                for h in range(nh):
                    s.op("tensor", lambda e, h=h: e.matmul(pun[:, 448 + h:449 + h], lhsT=ktm[:, h * 128:(h + 1) * 128], rhs=vw[:, h, 128:129], start=True, stop=True),
                         [ktmk, vwk], [kUn], inc=(h == nh - 1))
                    yield
                if outputs:
                    for h in range(nh):
                        s.op("tensor", lambda e, h=h: e.matmul(pn[:, h * 128:(h + 1) * 128], lhsT=qT[:, h, tsl], rhs=cgb[:, h * 128:(h + 1) * 128], start=True, stop=False),
                             [qtk, cgk], [kN], inc=False)
                        s.op("tensor", lambda e, h=h: e.matmul(pn[:, h * 128:(h + 1) * 128], lhsT=sm[:, h, :], rhs=vw[:, h, 0:128], start=False, stop=True),
                             [smk, vwk], [kN], inc=(h == nh - 1))
                        yield
                    for h in range(nh):
                        s.op("tensor", lambda e, h=h: e.matmul(pn[:, 448 + h:449 + h], lhsT=qT[:, h, tsl], rhs=ngb[:, h:h + 1], start=True, stop=False),
                             [qtk, cgk], [kD], inc=False)
                        s.op("tensor", lambda e, h=h: e.matmul(pn[:, 448 + h:449 + h], lhsT=sm[:, h, :], rhs=vw[:, h, 128:129], start=False, stop=True),
                             [smk, vwk], [kD], inc=(h == nh - 1))
                        yield
                    dn, dnk = scol(8)
                    s.op("scalar", lambda e, dn=dn: e.activation(out=dn[:, 0:nh], in_=pn[:, 448:448 + nh], func=AF.Abs), [kD], [dnk])
                    yield
                    s.op("vector", lambda e, gi=gi, dn=dn: e.tensor_tensor(out=dn[:, 0:nh], in0=dn[:, 0:nh], in1=el[:, gi, d, h0:h0 + nh], op=ALU.max), [dnk, "el"], [dnk])
                    yield
                    s.op("vector", lambda e, dn=dn: e.reciprocal(out=dn[:, 4:4 + nh], in_=dn[:, 0:nh]), [dnk], [dnk])
                    yield
                    yield from out_cb(d, ci, pn, kN, dn[:, 4:4 + nh], dnk)
                s.op("vector", lambda e: e.tensor_tensor(out=C, in0=C, in1=pu[:, 0:W], op=ALU.add), [Ck, kU], [Ck])
                yield
                s.op("vector", lambda e: e.tensor_tensor(out=nS, in0=nS, in1=pun[:, 448:448 + nh], op=ALU.add), [nk, kUn], [nk])
                yield

        spawn_list = []

        def run_interleaved(gens):
            gens = list(gens)
            while gens or spawn_list:
                gens.extend(spawn_list)
                del spawn_list[:]
                for g_ in list(gens):
                    try:
                        next(g_)
                    except StopIteration:
                        gens.remove(g_)

        winv = win_b.rearrange("(kc p) n -> p kc n", p=128)
        wsn = [0]

        def wload(wsl, col0, ncol):
            k = wsn[0] % len(wsl)
            wsn[0] += 1
            sl = wsl[k]
            s.dma("sync", f"wi{k}", [(sl[:, :, 0:ncol], winv[:, :, col0:col0 + ncol], ["win_b"], [("wsl", k)], {})])
            return sl, ("wsl", k)

        def proj_fm(wsl, col0, hT, hkey, ntok, pbank, pkey, coff=0):
            sl, slk = wsl
            for kc in range(8):
                s.op("tensor", lambda e, kc=kc: e.matmul(pbank[:, 0:ntok], lhsT=sl[:, kc, coff:coff + 128], rhs=hT[:, kc, 0:ntok], start=(kc == 0), stop=(kc == 7)),
                     [slk, hkey], [pkey], inc=(kc == 7))

        try:
          ckpt("P")
          with ExitStack() as sx:
              xc = T("xc", [128, 4, D], F32, sx)
              hTx = T("hTx", [128, 8, 512], BF16, sx)
              uT = T("uTx", [128, NJ, 512], BF16, sx)
              wd = T("wdx", [128, NJ, D], BF16, sx)
              wus = [T(f"wusx{i}", [128, 8, 512], BF16, sx) for i in range(2)]
              xnb = [T(f"xnbx{i}", [128, D], BF16, sx) for i in range(2)]
              sab = [T(f"sabx{i}", [128, 512], F32, sx) for i in range(2)]
              tmpb = [T(f"tmpbx{i}", [128, D], F32, sx) for i in range(2)]
              kTx = T("kTx", [128, 4, 512], BF16, sx)
              vxx = T("vxx", [128, 4, 4, 129], BF16, sx)
              gt8 = [T(f"gt8x_{i}", [8, 512], F32, sx) for i in range(4)]
              scx = [{"ktm": T(f"ktmx{i}", [128, 512], BF16, sx), "vw": T(f"vwx{i}", [128, 4, 129], BF16, sx)} for i in range(2)]
              ctxv = ctx_d.rearrange("(t p) d -> p t d", p=128)
              s.dma("sync", "xl", [(xc[:, t, :], ctxv[:, t, :], [], [("xc", t)], {}) for t in range(4)])
              s.dma("sync", "wd", [(wd[:, 0:11, :], w1d_b.rearrange("(kc p) n -> p kc n", p=128)[:, 0:11, :], ["w1d_b"], ["wdx"], {}),
                                   (wd[:, 11:22, :], w1d_b.rearrange("(kc p) n -> p kc n", p=128)[:, 11:22, :], ["w1d_b"], ["wdx"], {})])
              load_gate(0, 2)
              s.op("vector", lambda e: e.memset(vxx[:, :, :, 128:129], 1.0), [], ["vxx"])
              s.op("vector", lambda e: e.memset(Cst[:].rearrange("p a b c -> p (a b c)"), 0.0), [], ["Cst"])
              s.op("vector", lambda e: e.memset(nst[:].rearrange("p a b c -> p (a b c)"), 0.0), [], ["nst"])
              xt = [(xc[:, t, :], ("xc", t)) for t in range(4)]
              ffn_block(xt, 2, 0, w1u_b, "w1u_b", wd, "wdx", hTx, "hTx", uT, wus, xnb, sab, tmpb)
              ckpt("X1")
              prenorm_tiles([(xc[:, t, :], ("xc", t), t * 128) for t in range(4)], hTx, "hTx2", 1, 2, xnb)
              wsl = [T(f"wslx{i}", [128, 8, 512], BF16, sx) for i in range(2)]
              wk_sl = wload(wsl, 2048, 512)
              for h in range(4):
                  proj_fm(wk_sl, 0, hTx, "hTx2", 512, PF[h % 2], f"PF{h % 2}", coff=h * 128)
                  s.op("scalar", lambda e, h=h: e.activation(out=kTx[:, h, :], in_=PF[h % 2][:, :], func=AF.Copy, scale=HD ** -0.5), [f"PF{h % 2}"], ["kTx"])
              wv_sl, wvk = wload(wsl, 2560, 512)
              for t in range(4):
                  pb, pk = PF[2 + t % 2], f"PF{2 + t % 2}"
                  for kc in range(8):
                      s.op("tensor", lambda e, kc=kc, t=t, pb=pb: e.matmul(pb[:, :], lhsT=hTx[:, kc, t * 128:(t + 1) * 128], rhs=wv_sl[:, kc, :], start=(kc == 0), stop=(kc == 7)),
                           ["hTx2", wvk], [pk], inc=(kc == 7))
                  s.op("scalar", lambda e, t=t, pb=pb: e.copy(out=vxx[:, t, :, 0:128], in_=pb[:, :].rearrange("p (h x) -> p h x", h=4)), [pk], ["vxx"])
              ckpt("X2")
              gates_block8(hTx, "hTx2", 512, 0, gt8)
              ckpt("X3a")
              gates_split()
              ckpt("X3")
              for b in range(2):
                  av = lambda t, b=b: t[:, 2 * b:2 * b + 2, :, :]
                  rv = lambda t, b=b: t[:, :, 2 * b:2 * b + 2]
                  gates_finish(4, 2, [[0, 1], [1, 0]], av(atm), av(btm), rv(amax), rv(btot), mzero, mst[:, b, :], rv(Mrow), rv(growg), av(wk), av(el), av(gB), rv(tmp4))
                  s.barrier()
                  gens = []
                  for d in range(2):
                      chunks = [(2 * b + c, 2 * b + c) for c in ([0, 1] if d == 0 else [1, 0])]
                      gens.append(scan_gen(d, chunks, 4, 0, kTx, "kTx", vxx, "vxx", None, None, Cst[:, b, d, :], ("Cst", b, d), nst[:, b, d, :], ("nst", b, d), scx[d], False))
                  run_interleaved(gens)
              s.barrier()

          ckpt("X")
          for b in range(2):
              with ExitStack() as sb:
                  xs = T("xs", [128, NT, D], F32, sb)
                  xv = x_d[b].rearrange("(t p) d -> p t d", p=128)
                  for blk in range(NT // TB):
                      s.dma("sync", f"xl{blk}", [(xs[:, t, :], xv[:, t, :], [], [("xs", t)], {}) for t in range(blk * TB, (blk + 1) * TB)])
                  xtl = [(xs[:, t, :], ("xs", t)) for t in range(NT)]
                  with ExitStack() as sa_:
                      hT2 = [T(f"hTa{i}", [128, 8, 512], BF16, sa_) for i in range(2)]
                      uT = T("uTa", [128, NJ, 512], BF16, sa_)
                      wd = T("wda", [128, NJ, D], BF16, sa_)
                      wus = [T(f"wusa{i}", [128, 8, 512], BF16, sa_) for i in range(2)]
                      xnb = [T(f"xnba{i}", [128, D], BF16, sa_) for i in range(2)]
                      sab = [T(f"saba{i}", [128, 512], F32, sa_) for i in range(2)]
                      tmpb = [T(f"tmpba{i}", [128, D], F32, sa_) for i in range(2)]
                      wdv = w1d_b.rearrange("(kc p) n -> p kc n", p=128)
                      s.dma("sync", "wd", [(wd[:, 0:11, :], wdv[:, 0:11, :], ["w1d_b"], ["wda"], {}), (wd[:, 11:22, :], wdv[:, 11:22, :], ["w1d_b"], ["wda"], {})])
                      load_gate(0, b)
                      for blk in range(NT // TB):
                          hooks = None
                          if blk + 1 < NT // TB:
                              nt_ = [(xs[:, (blk + 1) * TB + t, :], ("xs", (blk + 1) * TB + t), t * 128) for t in range(TB)]
                              hooks = prenorm_hooks(nt_, hT2[(blk + 1) % 2], ("hTa", (blk + 1) % 2), 0, b, xnb, (2, 4, 6, 8))
                          ffn_block(xtl[blk * TB:(blk + 1) * TB], b, 0, w1u_b, "w1u_b", wd, "wda", hT2[blk % 2], ("hTa", blk % 2), uT, wus, xnb, sab, tmpb,
                                    pre_done=(blk > 0), hooks=hooks)
                      s.barrier()
                  ckpt("A")
                  if dbg and b == 0:
                      s.dma("sync", "dbg", [(dbg_out["dbg_x1"].rearrange("(t p) d -> p t d", p=128), xs[:], [("xs", t) for t in range(NT)], ["dbgx1"], {})])
                  with ExitStack() as sm_:
                      mixT = T("mixT", [128, 8, L], BF16, sm_)
                      for g in range(2):
                          hs = [2 * g, 2 * g + 1]
                          with ExitStack() as sg:
                              qT = T("qT", [128, 2, L], BF16, sg)
                              kT = T("kT", [128, 2, L], BF16, sg)
                              vx = T("vx", [128, NT, 2, 129], BF16, sg)
                              sgo = T("sgo", [128, NT, 256], BF16, sg)
                              s.op("vector", lambda e: e.memset(vx[:, :, :, 128:129], 1.0), [], ["vx"])
                              with ExitStack() as sB:
                                  hT2 = [T(f"hTb{i}", [128, 8, 512], BF16, sB) for i in range(2)]
                                  xnb = [T(f"xnbb{i}", [128, D], BF16, sB) for i in range(2)]
                                  wsl = [T(f"wslb{i}", [128, 8, 256], BF16, sB) for i in range(3)]
                                  cvt = [T(f"cvt{i}", [128, 512], F32, sB) for i in range(4)]
                                  if g == 0:
                                      gt8 = [T(f"gt8b_{i}", [8, 512], F32, sB) for i in range(4)]
                                  th = T("th", [128, 256], F32, sB)
                                  for blk in range(NT // TB):
                                      hT, hkey = hT2[blk % 2], ("hTb", blk % 2)
                                      tok0 = blk * 512
                                      if blk == 0:
                                          prenorm_tiles([(xs[:, blk * TB + t, :], ("xs", blk * TB + t), t * 128) for t in range(TB)], hT, hkey, 1, b, xnb)
                                      hk_ = {}
                                      if blk + 1 < NT // TB:
                                          nt_ = [(xs[:, (blk + 1) * TB + t, :], ("xs", (blk + 1) * TB + t), t * 128) for t in range(TB)]
                                          hk_ = prenorm_hooks(nt_, hT2[(blk + 1) % 2], ("hTb", (blk + 1) % 2), 1, b, xnb, (0, 1, 2, 3))
                                      wq = wload(wsl, 1536 + hs[0] * 128, 256)
                                      for hh in range(2):
                                          proj_fm(wq, 0, hT, hkey, 512, PF[hh], f"PF{hh}", coff=hh * 128)
                                          s.op("scalar", lambda e, hh=hh: e.copy(out=qT[:, hh, tok0:tok0 + 512], in_=PF[hh][:, :]), [f"PF{hh}"], ["qT"])
                                      if 0 in hk_:
                                          hk_[0]()
                                      wkk = wload(wsl, 2048 + hs[0] * 128, 256)
                                      for hh in range(2):
                                          proj_fm(wkk, 0, hT, hkey, 512, PF[hh], f"PF{hh}", coff=hh * 128)
                                          s.op("scalar", lambda e, hh=hh: e.activation(out=kT[:, hh, tok0:tok0 + 512], in_=PF[hh][:, :], func=AF.Copy, scale=HD ** -0.5), [f"PF{hh}"], ["kT"])
                                      if 1 in hk_:
                                          hk_[1]()
                                      wb = wload(wsl, hs[0] * 128, 256)
                                      wc = wload(wsl, 512 + hs[0] * 128, 256)
                                      wu_ = wload(wsl, 1024 + hs[0] * 128, 256)
                                      for jj in range(2):
                                          j = 2 * g + jj
                                          pb0, pb1, pb2 = PF[3 * jj], PF[3 * jj + 1], PF[3 * jj + 2]
                                          k0, k1, k2 = f"PF{3 * jj}", f"PF{3 * jj + 1}", f"PF{3 * jj + 2}"
                                          proj_fm(wc, 0, hT, hkey, 512, pb1, k1, coff=jj * 128)
                                          proj_fm(wu_, 0, hT, hkey, 512, pb2, k2, coff=jj * 128)
                                          proj_fm(wb, 0, hT, hkey, 512, pb0, k0, coff=jj * 128)
                                          z_, cz_ = cvt[2 * jj], cvt[2 * jj + 1]
                                          zk, czk = ("cvt", 2 * jj), ("cvt", 2 * jj + 1)
                                          s.op("scalar", lambda e, cz_=cz_, pb1=pb1: e.copy(out=cz_[:], in_=pb1[:, :]), [k1], [czk])
                                          s.op("vector", lambda e, z_=z_, cz_=cz_, pb2=pb2: e.tensor_tensor(out=z_[:], in0=pb2[:, :], in1=cz_[:], op=ALU.mult), [k2, czk], [zk])
                                          s.op("gpsimd", lambda e, j=j, z_=z_, cz_=cz_: e.tensor_scalar(out=cz_[:], in0=z_[:], scalar1=cwT[:, j * 3 + 1:j * 3 + 2], scalar2=None, op0=ALU.mult), [zk, "cwT"], [czk])
                                          z3 = z_[:].rearrange("p (r w) -> p r w", w=64)
                                          c3 = cz_[:].rearrange("p (r w) -> p r w", w=64)
                                          s.op("vector", lambda e, j=j, z3=z3, c3=c3: e.scalar_tensor_tensor(out=c3[:, :, 1:64], in0=z3[:, :, 0:63], scalar=cwT[:, j * 3:j * 3 + 1], in1=c3[:, :, 1:64], op0=ALU.mult, op1=ALU.add),
                                               [zk, czk, "cwT"], [czk])
                                          s.op("vector", lambda e, j=j, z3=z3, c3=c3: e.scalar_tensor_tensor(out=c3[:, :, 0:63], in0=z3[:, :, 1:64], scalar=cwT[:, j * 3 + 2:j * 3 + 3], in1=c3[:, :, 0:63], op0=ALU.mult, op1=ALU.add),
                                               [zk, czk, "cwT"], [czk])
                                          s.op("vector", lambda e, j=j, cz_=cz_, pb0=pb0: e.tensor_tensor(out=mixT[:, j, tok0:tok0 + 512], in0=pb0[:, :], in1=cz_[:], op=ALU.mult), [k0, czk], [("mixT", j)])
                                          if (2 + jj) in hk_:
                                              hk_[2 + jj]()
                                      wv = wload(wsl, 2560 + hs[0] * 128, 256)
                                      wo = wload(wsl, 3072 + hs[0] * 128, 256)
                                      for t in range(TB):
                                          ci = blk * TB + t
                                          pb, pk = PF[3 + t % 2], f"PF{3 + t % 2}"
                                          for kc in range(8):
                                              s.op("tensor", lambda e, kc=kc, t=t, pb=pb: e.matmul(pb[:, 0:256], lhsT=hT[:, kc, t * 128:(t + 1) * 128], rhs=wv[0][:, kc, :], start=(kc == 0), stop=(kc == 7)),
                                                   [hkey, wv[1]], [pk], inc=(kc == 7))
                                          for kc in range(8):
                                              s.op("tensor", lambda e, kc=kc, t=t, pb=pb: e.matmul(pb[:, 256:512], lhsT=hT[:, kc, t * 128:(t + 1) * 128], rhs=wo[0][:, kc, :], start=(kc == 0), stop=(kc == 7)),
                                                   [hkey, wo[1]], [pk], inc=(kc == 7))
                                          s.op("scalar", lambda e, ci=ci, pb=pb: e.copy(out=vx[:, ci, :, 0:128], in_=pb[:, 0:256].rearrange("p (h x) -> p h x", h=2)), [pk], ["vx"])
                                          s.op("scalar", lambda e, pb=pb: e.activation(out=th[:], in_=pb[:, 256:512], func=AF.Tanh, scale=0.5), [pk], ["th"])
                                          s.op("vector", lambda e, ci=ci: e.scalar_tensor_tensor(out=sgo[:, ci, :], in0=th[:], scalar=1.0, in1=mhnb[:, hs[0] * 128:hs[0] * 128 + 256], op0=ALU.add, op1=ALU.mult),
                                               ["th", "mhnb"], ["sgo"])
                                      if g == 0:
                                          gates_block8(hT, hkey, 512, blk * TB, gt8)
                                  if g == 0:
                                      gates_split()
                                      gates_finish(4, NT, [list(range(NT)), list(range(NT - 1, -1, -1))], atm, btm, amax, btot, mst[:, b, :], mdump, Mrow, growg, wk, el, gB, tmp4)
                                  s.barrier()
                              ckpt("B")
                              with ExitStack() as sC:
                                  scd = [{"ktm": T(f"ktm{i}", [128, 256], BF16, sC), "vw": T(f"vw{i}", [128, 2, 129], BF16, sC),
                                          "cgb": T(f"cgb{i}", [128, 256], BF16, sC), "ngb": T(f"ngb{i}", [128, 2], BF16, sC),
                                          "sm": T(f"sm{i}", [128, 2, 128], BF16, sC)} for i in range(2)]
                                  hst = T("hst", [128, NT, 256], BF16, sC)
                                  Cw = T("Cw", [128, 2, 256], F32, sC)
                                  nw = T("nw", [128, 2, 2], F32, sC)
                                  hsum = [[T(f"hsum{i}_{p}", [128, 256], F32, sC) for p in range(2)] for i in range(2)]
                                  hsq = [T(f"hsq{i}", [128, 256], F32, sC) for i in range(2)]
                                  hn = [T(f"hn{i}", [128, 256], BF16, sC) for i in range(2)]
                                  for d in range(2):
                                      s.op("vector", lambda e, d=d: e.tensor_copy(out=Cw[:, d, :], in_=Cst[:, b, d, hs[0] * 128:hs[0] * 128 + 256]), [("Cst", b, d)], [("Cw", d)])
                                      s.op("vector", lambda e, d=d: e.tensor_copy(out=nw[:, d, :], in_=nst[:, b, d, hs[0]:hs[0] + 2]), [("nst", b, d)], [("nw", d)])
                                  first_half = {0: set(range(0, NT // 2)), 1: set(range(NT // 2, NT))}

                                  def out_cb(d, ci, pn, pnk, rden, rk):
                                      if ci in first_half[d]:
                                          s.op("vector", lambda e: e.tensor_tensor(out=hst[:, ci, :].rearrange("p (h x) -> p h x", h=2), in0=pn[:, 0:256].rearrange("p (h x) -> p h x", h=2),
                                                                                   in1=rden.unsqueeze(2).to_broadcast([128, 2, 128]), op=ALU.mult), [pnk, rk], [("hst", ci)])
                                          yield
                                          return
                                      par = ci % 2
                                      hsm, hk = hsum[d][par], ("hsum", d, par)
                                      s.op("vector", lambda e: e.tensor_tensor(out=hsm[:].rearrange("p (h x) -> p h x", h=2), in0=pn[:, 0:256].rearrange("p (h x) -> p h x", h=2),
                                                                               in1=rden.unsqueeze(2).to_broadcast([128, 2, 128]), op=ALU.mult), [pnk, rk], [hk])
                                      yield
                                      spawn_list.append(combine_rest(d, ci, hsm, hk))

                                  def combine_rest(d, ci, hsm, hk):
                                      sq, sqk = hsq[d], ("hsq", d)
                                      s.op("gpsimd", lambda e: e.tensor_tensor(out=hsm[:], in0=hsm[:], in1=hst[:, ci, :], op=ALU.add), [hk, ("hst", ci)], [hk])
                                      yield
                                      st_, stk = scol(8)
                                      s.op("gpsimd", lambda e: e.tensor_tensor(out=sq[:], in0=hsm[:], in1=hsm[:], op=ALU.mult), [hk], [sqk])
                                      yield
                                      s.op("vector", lambda e: e.tensor_reduce(out=st_[:, 0:2], in_=sq[:].rearrange("p (h x) -> p h x", h=2), axis=AX.X, op=ALU.add), [sqk], [stk])
                                      yield
                                      s.op("vector", lambda e: e.tensor_scalar(out=st_[:, 2:4], in0=st_[:, 0:2], scalar1=1.0 / HD, scalar2=EPS, op0=ALU.mult, op1=ALU.add), [stk], [stk])
                                      yield
                                      s.op("gpsimd", lambda e: e.tensor_tensor(out=st_[:, 4:6], in0=st_[:, 2:4], in1=mhalf[:, 0:2], op=ALU.pow), [stk, "mhalf"], [stk])
                                      yield
                                      s.op("vector", lambda e: e.tensor_tensor(out=hsm[:].rearrange("p (h x) -> p h x", h=2), in0=hsm[:].rearrange("p (h x) -> p h x", h=2),
                                                                               in1=st_[:, 4:6].unsqueeze(2).to_broadcast([128, 2, 128]), op=ALU.mult), [hk, stk], [hk])
                                      yield
                                      hnn, hnk = hn[d], ("hn", d)
                                      s.op("gpsimd", lambda e: e.tensor_tensor(out=hnn[:], in0=hsm[:], in1=sgo[:, ci, :], op=ALU.mult), [hk, "sgo"], [hnk])
                                      yield
                                      pt, ptk = PT[d], ("ptK", d)
                                      for hh in range(2):
                                          s.op("tensor", lambda e, hh=hh: e.transpose(out=pt[:, 512 + hh * 128:512 + (hh + 1) * 128], in_=hnn[:, hh * 128:(hh + 1) * 128], identity=identb[:]),
                                               [hnk, "identb"], [ptk], inc=(hh == 1))
                                      yield
                                      s.op("scalar", lambda e: e.copy(out=mixT[:, 4 + hs[0]:4 + hs[0] + 2, ci * 128:(ci + 1) * 128], in_=pt[:, 512:768].rearrange("p (h x) -> p h x", h=2)),
                                           [ptk], [("mixT", 4 + hs[0], ci), ("mixT", 5 + hs[0], ci)])
                                      yield

                                  gens = [scan_gen(0, [(c, c) for c in range(NT)], 2, hs[0], kT, "kT", vx, "vx", qT, "qT", Cw[:, 0, :], ("Cw", 0), nw[:, 0, :], ("nw", 0), scd[0], True, out_cb),
                                          scan_gen(1, [(c, c) for c in range(NT - 1, -1, -1)], 2, hs[0], kT, "kT", vx, "vx", qT, "qT", Cw[:, 1, :], ("Cw", 1), nw[:, 1, :], ("nw", 1), scd[1], True, out_cb)]
                                  run_interleaved(gens)
                                  for jx in (4 + hs[0], 5 + hs[0]):
                                      s.lastw[("mixT", jx)] = ("scalar", s.cnt["scalar"])
                                      s.reads[("mixT", jx)] = []
                                  s.barrier()
                      ckpt("C")
                      if dbg and b == 0:
                          s.dma("sync", "dbg", [(dbg_out["dbg_mix"], mixT[:].rearrange("p a b -> p (a b)"), [("mixT", j) for j in range(8)], ["dbgmix"], {})])
                      with ExitStack() as s2:
                          wo_t = T("wo_t", [128, 8, D], BF16, s2)
                          tmpb = [T(f"tmpbc{i}", [128, D], F32, s2) for i in range(2)]
                          s.dma("sync", "wo", [(wo_t[:], wout_b.rearrange("(kc p) n -> p kc n", p=128), ["wout_b"], ["wo_t"], {})])
                          load_gate(1, b)
                          for t in range(NT):
                              p0 = (t % 3) * 2
                              py = [PF[p0], PF[p0 + 1]]
                              pyk = [f"PF{p0}", f"PF{p0 + 1}"]
                              for hf in range(2):
                                  for kc in range(8):
                                      s.op("tensor", lambda e, kc=kc, hf=hf, t=t: e.matmul(py[hf][:, :], lhsT=mixT[:, kc, t * 128:(t + 1) * 128], rhs=wo_t[:, kc, hf * 512:(hf + 1) * 512],
                                                                                            start=(kc == 0), stop=(kc == 7)),
                                           [("mixT", kc), "wo_t"], [pyk[hf]], inc=(kc == 7))
                              postnorm_residual(py, pyk, xs[:, t, :], ("xs", t), tmpb)
                          s.barrier()
                  if dbg and b == 0:
                      s.dma("sync", "dbg", [(dbg_out["dbg_x2"].rearrange("(t p) d -> p t d", p=128), xs[:], [("xs", t) for t in range(NT)], ["dbgx2"], {})])
                  with ExitStack() as sd_:
                      hT2 = [T(f"hTd{i}", [128, 8, 512], BF16, sd_) for i in range(2)]
                      uT = T("uTd", [128, NJ, 512], BF16, sd_)
                      wd = T("wdd", [128, NJ, D], BF16, sd_)
                      wus = [T(f"wusd{i}", [128, 8, 512], BF16, sd_) for i in range(2)]
                      xnb = [T(f"xnbd{i}", [128, D], BF16, sd_) for i in range(2)]
                      sab = [T(f"sabd{i}", [128, 512], F32, sd_) for i in range(2)]
                      tmpb = [T(f"tmpbd{i}", [128, D], F32, sd_) for i in range(2)]
                      wdv = w2d_b.rearrange("(kc p) n -> p kc n", p=128)
                      s.dma("sync", "wd", [(wd[:, 0:11, :], wdv[:, 0:11, :], ["w2d_b"], ["wdd"], {}), (wd[:, 11:22, :], wdv[:, 11:22, :], ["w2d_b"], ["wdd"], {})])
                      load_gate(2, b)
                      ov = out_d[b].rearrange("(t p) d -> t p d", p=128)
                      for blk in range(NT // TB):
                          hooks = None
                          if blk + 1 < NT // TB:
                              nt_ = [(xs[:, (blk + 1) * TB + t, :], ("xs", (blk + 1) * TB + t), t * 128) for t in range(TB)]
                              hooks = prenorm_hooks(nt_, hT2[(blk + 1) % 2], ("hTd", (blk + 1) % 2), 2, b, xnb, (2, 4, 6, 8))
                          ffn_block(xtl[blk * TB:(blk + 1) * TB], b, 2, w2u_b, "w2u_b", wd, "wdd", hT2[blk % 2], ("hTd", blk % 2), uT, wus, xnb, sab, tmpb,
                                    store=[ov[blk * TB + t] for t in range(TB)], pre_done=(blk > 0), hooks=hooks)
                      s.barrier()
        except _Stop:
            pass
        s.barrier()
        print("ops", s.nops, "waits", s.nwaits, flush=True)
    return nc


_CACHE = {}


def _prep_inputs(x, c, ctx, c_ctx, w_mod, b_mod, norm_g, ffn1_up, ffn1_down, ffn2_up, ffn2_down,
                 w_in, b_gates, conv_w, mh_norm, w_out):
    f = lambda a: np.ascontiguousarray(np.asarray(a, dtype=np.float32))

    def perm_up(w):
        w = f(w)
        return np.ascontiguousarray(w.reshape(D, 2, NJ, 128).transpose(0, 2, 1, 3).reshape(D, 2 * FF))

    w_in = f(w_in[0])
    shared = {
        "w_mod": f(w_mod[0]),
        "b_mod3": np.ascontiguousarray(np.broadcast_to(f(b_mod[0])[None, :], (3, 9 * D))),
        "norm_g3": np.ascontiguousarray(np.broadcast_to(f(norm_g[0]).reshape(1, 6 * D), (3, 6 * D))),
        "w1u": perm_up(ffn1_up[0]), "w1d": f(ffn1_down[0]),
        "w2u": perm_up(ffn2_up[0]), "w2d": f(ffn2_down[0]),
        "w_in_main": np.ascontiguousarray(w_in[:, :3584]),
        "w_in_g": np.ascontiguousarray(w_in[:, 3584:3600]),
        "bgT": np.ascontiguousarray(f(b_gates[0]).reshape(4, 4).T),
        "bg8": np.ascontiguousarray(f(b_gates[0]).reshape(2, 8).T),
        "conv_wT": np.ascontiguousarray(f(conv_w[0]).T.reshape(4, 128, 3).transpose(1, 0, 2).reshape(128, 12)),
        "mhn_b": np.ascontiguousarray(np.broadcast_to(f(mh_norm[0])[None, :], (128, 512))),
        "w_out": f(w_out[0]),
        "consts": _consts_np(),
    }
    x = np.asarray(x, dtype=np.float32)
    ctx = np.asarray(ctx, dtype=np.float32)
    c = np.asarray(c, dtype=np.float32)
    c_ctx = np.asarray(c_ctx, dtype=np.float32)
    in_maps = []
    for i in range(NCORES):
        m = dict(shared)
        m["x"] = np.ascontiguousarray(x[2 * i:2 * i + 2])
        m["ctx"] = np.ascontiguousarray(ctx[2 * i:2 * i + 2].reshape(2 * CL, D))
        m["cc"] = np.ascontiguousarray(np.stack([c[2 * i], c[2 * i + 1], c_ctx], 0))
        in_maps.append(m)
    return in_maps


def kernel(**inputs):
    in_maps = _prep_inputs(**inputs)
    if "nc" not in _CACHE:
        _CACHE["nc"] = build_program()
    nc = _CACHE["nc"]
    res = run_bass_kernel_spmd(nc, in_maps, core_ids=list(range(NCORES)))
    out = np.concatenate([np.asarray(r["out"]) for r in res.results], axis=0)
    return out.astype(np.float32)
```

```python
import os
import numpy as np
import concourse.bass as bass
import concourse.mybir as mybir
from concourse.bass_utils import run_bass_kernel_spmd
from concourse.alu_op_type import AluOpType as ALU
from contextlib import ExitStack

F32 = mybir.dt.float32
BF16 = mybir.dt.bfloat16
AF = mybir.ActivationFunctionType
AX = mybir.AxisListType

ENGS = ("tensor", "vector", "scalar", "gpsimd", "sync")

D = 1024
FF = 2816
NJ = 22
L = 2048
NT = 16
CL = 256
NH = 4
HD = 128
EPS = 1e-6
NCORES = 8
TB = 4


class Sched:
    def __init__(self, nc, stack, same_engine_sync=True):
        self.nc = nc
        self.cnt = {e: 0 for e in ENGS}
        self.seen = {e: {} for e in ENGS}
        self.lastw = {}
        self.reads = {}
        self.sems = {}
        self.stack = stack
        self.same = same_engine_sync
        self.dmacnt = {}
        for e in ENGS:
            self.sems[e] = stack.enter_context(nc.semaphore("p_" + e))
        self.nwaits = 0
        self.nops = 0
        self.dead = False
        self.E = {"tensor": nc.tensor, "vector": nc.vector, "scalar": nc.scalar,
                  "gpsimd": nc.gpsimd, "sync": nc.sync}

    def _sem(self, key):
        if key not in self.sems:
            self.sems[key] = self.stack.enter_context(self.nc.semaphore("d_" + str(key)))
            self.dmacnt[key] = 0
        return self.sems[key]

    def _deps(self, eng, reads, writes):
        need = {}

        def req(tok, war=False):
            if tok is None:
                return
            k, v = tok
            if k == eng:
                if war or not self.same or v > self.cnt[eng]:
                    return
            if need.get(k, 0) < v:
                need[k] = v

        for r in reads:
            req(self.lastw.get(r))
        for w in writes:
            req(self.lastw.get(w))
            for t in self.reads.get(w, ()):
                req(t, war=True)
        for k, v in need.items():
            if self.seen[eng].get(k, 0) >= v:
                continue
            if k in self.cnt:
                assert v <= self.cnt[k], f"dep on unsignalled op: {eng} needs {k}>={v}, issued {self.cnt[k]}"
            else:
                assert v <= self.dmacnt[k], f"dep on unissued dma {k} {v}"
            self.seen[eng][k] = v
            self.E[eng].wait_ge(self.sems[k], v)
            self.nwaits += 1

    def _record(self, tok, reads, writes):
        for r in reads:
            self.reads.setdefault(r, []).append(tok)
        for w in writes:
            self.lastw[w] = tok
            self.reads[w] = []

    def op(self, eng, fn, reads=(), writes=(), inc=True):
        if self.dead:
            return
        self._deps(eng, reads, writes)
        tok = (eng, self.cnt[eng] + 1)
        if inc:
            self.cnt[eng] += 1
        ins = fn(self.E[eng])
        if inc:
            ins.then_inc(self.sems[eng], 1)
        self.nops += 1
        self._record(tok, reads, writes)

    def dma(self, eng, semkey, items):
        if self.dead:
            return None
        self._sem(semkey)
        allr = [r for it in items for r in it[2]]
        allw = [w for it in items for w in it[3]]
        self._deps(eng, allr, allw)
        self.dmacnt[semkey] += 16 * len(items)
        tok = (semkey, self.dmacnt[semkey])
        for (o, i, r, w, kw) in items:
            self.E[eng].dma_start(out=o, in_=i, **kw).then_inc(self.sems[semkey], 16)
        self._record(tok, allr, allw)
        return tok

    def wait_all(self, eng, exclude=()):
        if self.dead:
            return
        for k in list(self.cnt) + list(self.dmacnt):
            if k in exclude:
                continue
            v = self.cnt[k] if k in self.cnt else self.dmacnt[k]
            if k == eng or v == 0:
                continue
            if self.seen[eng].get(k, 0) >= v:
                continue
            self.seen[eng][k] = v
            self.E[eng].wait_ge(self.sems[k], v)

    def barrier(self, exclude=()):
        for e in ENGS:
            self.wait_all(e, exclude)


def _consts_np():
    c = np.zeros((128, 1024), np.float32)
    i = np.arange(128)
    c[:, 0:128] = np.eye(128)
    c[:, 128:256] = (i[:, None] <= i[None, :])
    c[:, 256:384] = (i[:, None] >= i[None, :])
    c[:, 384:512] = 1.0
    for r in range(3):
        c[r, 512 + r * 128: 512 + (r + 1) * 128] = 1.0
    c[0:4, 896:900] = np.eye(4)
    c[0:3, 904:907] = np.eye(3)
    c[0:4, 908] = 1.0
    c[4:8, 908] = -1.0
    c[4:8, 909] = 1.0
    return c


class _Stop(Exception):
    pass


def build_program(dbg=False, stop=None):
    nc = bass.Bass("TRN2", target_bir_lowering=False)

    sref = []

    def ckpt(name):
        if stop == name and not sref[0].dead:
            sref[0].barrier()
            sref[0].dead = True

    def din(name, shape, dt=F32):
        return nc.dram_tensor(name, list(shape), dt, kind="ExternalInput").ap()

    x_d = din("x", [2, L, D])
    ctx_d = din("ctx", [2 * CL, D])
    cc_d = din("cc", [3, D])
    wmod_d = din("w_mod", [D, 9 * D])
    bmod_d = din("b_mod3", [3, 9 * D])
    ng_d = din("norm_g3", [3, 6 * D])
    w1u_d = din("w1u", [D, 2 * FF])
    w1d_d = din("w1d", [FF, D])
    w2u_d = din("w2u", [D, 2 * FF])
    w2d_d = din("w2d", [FF, D])
    win_d = din("w_in_main", [D, 3584])
    wing_d = din("w_in_g", [D, 16])
    bg_d = din("bgT", [4, 4])
    bg8_d = din("bg8", [8, 2])
    cw_d = din("conv_wT", [128, 12])
    mhn_d = din("mhn_b", [128, 512])
    wout_d = din("w_out", [D, D])
    const_d = din("consts", [128, 1024])
    out_d = nc.dram_tensor("out", [2, L, D], F32, kind="ExternalOutput").ap()

    def dscr(name, shape, dt):
        return nc.dram_tensor(name, list(shape), dt, kind="Internal").ap()

    w1u_b = dscr("w1u_b", [D, 2 * FF], BF16)
    w1d_b = dscr("w1d_b", [FF, D], BF16)
    w2u_b = dscr("w2u_b", [D, 2 * FF], BF16)
    w2d_b = dscr("w2d_b", [FF, D], BF16)
    win_b = dscr("win_b", [D, 3584], BF16)
    wout_b = dscr("wout_b", [D, D], BF16)
    grow_d = dscr("grow_d", [3, 3 * D], F32)

    dbg_out = {}
    if dbg:
        dbg_out["dbg_x1"] = nc.dram_tensor("dbg_x1", [L, D], F32, kind="ExternalOutput").ap()
        dbg_out["dbg_mix"] = nc.dram_tensor("dbg_mix", [128, 8 * L], BF16, kind="ExternalOutput").ap()
        dbg_out["dbg_x2"] = nc.dram_tensor("dbg_x2", [L, D], F32, kind="ExternalOutput").ap()

    with ExitStack() as st:
        s = Sched(nc, st)
        sref.append(s)

        tcount = [0]

        def T(name, shape, dt, stack=None):
            tcount[0] += 1
            return (stack or st).enter_context(nc.sbuf_tensor(f"s{tcount[0]}_{name}", list(shape), dt))

        PT = [st.enter_context(nc.psum_tensor(f"PT{i}", [128, 1024], BF16)) for i in range(2)]
        PF = [st.enter_context(nc.psum_tensor(f"PF{i}", [128, 512], F32)) for i in range(6)]

        cst = T("cst", [128, 1024], F32)
        identb = T("identb", [128, 128], BF16)
        maskb = T("maskb", [128, 2, 128], F32)
        modT = T("modT", [128, 144], F32)
        gtile = T("gtile", [128, D], F32)
        mhalf = T("mhalf", [128, 8], F32)
        wing = T("wing", [128, 8, 16], BF16)
        bgT = T("bgT", [4, 4], F32)
        mdump = T("mdump", [4, 2], F32)
        cwT = T("cwT", [128, 12], F32)
        mhnb = T("mhnb", [128, 512], F32)
        Cst = T("Cst", [128, 2, 2, 512], F32)
        nst = T("nst", [128, 2, 2, 4], F32)
        mst = T("mst", [4, 2, 2], F32)
        stat = T("stat", [128, 256], F32)
        junk = [T(f"junk{i}", [128, D], BF16) for i in range(1)]

        statn = [0]

        def scol(n=1):
            if statn[0] + n > 256:
                statn[0] = 0
            c0 = statn[0]
            statn[0] += n
            return stat[:, c0:c0 + n], ("stat", c0)

        ident_f = cst[:, 0:128]
        ones_f = cst[:, 384:512]

        s.dma("sync", "c0", [(cst[:], const_d, [], ["cst"], {})])
        s.dma("sync", "c1", [(bgT[:], bg_d, [], ["bgT"], {}), (cwT[:], cw_d, [], ["cwT"], {}),
                             (mhnb[:], mhn_d, [], ["mhnb"], {})])
        s.op("vector", lambda e: e.tensor_copy(out=identb[:], in_=cst[:, 0:128]), ["cst"], ["identb"])
        s.op("vector", lambda e: e.tensor_copy(out=maskb[:].rearrange("p a b -> p (a b)"), in_=cst[:, 128:384]), ["cst"], ["maskb"])
        s.op("vector", lambda e: e.memset(mhalf[:], -0.5), [], ["mhalf"])
        s.op("vector", lambda e: e.tensor_scalar(out=mhnb[:], in0=mhnb[:], scalar1=0.5, scalar2=None, op0=ALU.mult), ["mhnb"], ["mhnb"])
        s.dma("gpsimd", "cv1", [(w1u_b, w1u_d, [], ["w1u_b"], {})])
        s.dma("gpsimd", "cv2", [(w1d_b, w1d_d, [], ["w1d_b"], {})])
        wingv = wing_d.rearrange("(kc p) n -> p kc n", p=128)
        s.dma("gpsimd", "cv3", [(wing[:], wingv, [], ["wing"], {})])

        with ExitStack() as sp:
            cc = T("cc", [3, D], F32, sp)
            scc = T("scc", [3, D], F32, sp)
            sccT = T("sccT", [128, 8, 3], BF16, sp)
            mrow = T("mrow", [3, 9 * D], F32, sp)
            ng3 = T("ng3", [3, 6 * D], F32, sp)
            grow = T("grow", [3, 3 * D], F32, sp)
            arow = T("arow", [3, 3 * D], F32, sp)
            wmt = [T(f"wmt{i}", [128, 8, 512], BF16, sp) for i in range(3)]
            s.dma("sync", "c2", [(cc[:], cc_d, [], ["cc"], {}), (mrow[:], bmod_d, [], ["mrow"], {}),
                                 (ng3[:], ng_d, [], ["ng3"], {})])
            s.op("scalar", lambda e: e.activation(out=scc[:], in_=cc[:], func=AF.Silu), ["cc"], ["scc"])
            for kc in range(8):
                s.op("tensor", lambda e, kc=kc: e.matmul(PF[0][:, kc * 3:(kc + 1) * 3], lhsT=scc[0:3, kc * 128:(kc + 1) * 128],
                                                          rhs=cst[0:3, 904:907], start=True, stop=True),
                     ["scc", "cst"], ["PF0"], inc=(kc == 7))
            s.op("vector", lambda e: e.tensor_copy(out=sccT[:].rearrange("p a b -> p (a b)"), in_=PF[0][:, 0:24]), ["PF0"], ["sccT"])
            wmv = wmod_d.rearrange("(kc p) n -> p kc n", p=128)
            for n in range(18):
                sl = wmt[n % 3]
                key = ("wmt", n % 3)
                s.dma("gpsimd", f"wm{n % 3}", [(sl[:], wmv[:, :, n * 512:(n + 1) * 512], [], [key], {})])
                pb = PF[1 + (n % 2)]
                pk = f"PF{1 + (n % 2)}"
                for kc in range(8):
                    s.op("tensor", lambda e, kc=kc, sl=sl, pb=pb: e.matmul(pb[0:3, :], lhsT=sccT[:, kc, :], rhs=sl[:, kc, :],
                                                                            start=(kc == 0), stop=(kc == 7)),
                         ["sccT", key], [pk], inc=(kc == 7))
                s.op("vector", lambda e, n=n, pb=pb: e.tensor_tensor(out=mrow[:, n * 512:(n + 1) * 512], in0=pb[0:3, :],
                                                                      in1=mrow[:, n * 512:(n + 1) * 512], op=ALU.add),
                     [pk, "mrow"], ["mrow"])
            s.dma("gpsimd", "cv4", [(win_b, win_d, [], ["win_b"], {})])
            s.dma("gpsimd", "cv5", [(wout_b, wout_d, [], ["wout_b"], {})])
            s.dma("gpsimd", "cv6", [(w2u_b, w2u_d, [], ["w2u_b"], {})])
            s.dma("gpsimd", "cv7", [(w2d_b, w2d_d, [], ["w2d_b"], {})])
            for i in range(3):
                coef = 1.0 if i == 1 else 0.5
                s.op("vector", lambda e, i=i: e.scalar_tensor_tensor(out=arow[:, i * D:(i + 1) * D], in0=mrow[:, (3 * i + 1) * D:(3 * i + 2) * D],
                                                                      scalar=1.0, in1=ng3[:, (2 * i) * D:(2 * i + 1) * D], op0=ALU.add, op1=ALU.mult),
                     ["mrow", "ng3"], ["arow"])
                s.op("vector", lambda e, i=i, coef=coef: e.scalar_tensor_tensor(out=grow[:, i * D:(i + 1) * D], in0=mrow[:, (3 * i + 2) * D:(3 * i + 3) * D],
                                                                                  scalar=coef, in1=ng3[:, (2 * i + 1) * D:(2 * i + 2) * D], op0=ALU.mult, op1=ALU.mult),
                     ["mrow", "ng3"], ["grow"])
            s.dma("sync", "c3", [(grow_d, grow[:], ["grow"], ["grow_d"], {})])
            for i in range(3):
                for a_s in range(2):
                    for kc in range(8):
                        col = ((i * 2 + a_s) * 8 + kc) * 3
                        src = arow[0:3, i * D + kc * 128: i * D + (kc + 1) * 128] if a_s == 0 else mrow[0:3, 3 * i * D + kc * 128: 3 * i * D + (kc + 1) * 128]
                        last = (i == 2 and a_s == 1 and kc == 7)
                        s.op("tensor", lambda e, col=col, src=src: e.matmul(PF[3][:, col:col + 3], lhsT=src, rhs=cst[0:3, 904:907], start=True, stop=True),
                             ["arow", "mrow", "cst"], ["PF3"], inc=last)
            s.op("vector", lambda e: e.tensor_copy(out=modT[:], in_=PF[3][:, 0:144]), ["PF3"], ["modT"])
            s.barrier(exclude=("cv4", "cv5", "cv6", "cv7"))

        def mod_AS(i, kc, r):
            a = ((i * 2 + 0) * 8 + kc) * 3 + r
            b = ((i * 2 + 1) * 8 + kc) * 3 + r
            return modT[:, a:a + 1], modT[:, b:b + 1]

        def load_gate(i, r):
            src = grow_d[r:r + 1, i * D:(i + 1) * D].partition_broadcast(128)
            s.dma("sync", "gt", [(gtile[:], src, ["grow_d"], ["gtile"], {})])

        rr = {"pt": 0, "xn": 0, "jk": 0}

        def prenorm_a(grp, xnb):
            st_ = []
            for (xap, xkey, col0) in grp:
                ss, ssk = scol(3)
                xn = xnb[rr["xn"] % 2]
                xnk = ("xn", rr["xn"] % 2)
                rr["xn"] += 1
                pt = PT[rr["pt"] % 2]
                ptk = f"PT{rr['pt'] % 2}"
                rr["pt"] += 1
                st_.append((xap, xkey, col0, ss, ssk, xn, xnk, pt, ptk))
            for (xap, xkey, col0, ss, ssk, xn, xnk, pt, ptk) in st_:
                jk = junk[0]
                jkk = ("junk", 0)
                s.op("scalar", lambda e, jk=jk, xap=xap, ss=ss: e.activation(out=jk[:], in_=xap, func=AF.Square, accum_out=ss[:, 0:1]), [xkey], [ssk, jkk])
            for (xap, xkey, col0, ss, ssk, xn, xnk, pt, ptk) in st_:
                s.op("vector", lambda e, ss=ss: e.tensor_scalar(out=ss[:, 1:2], in0=ss[:, 0:1], scalar1=1.0 / D, scalar2=EPS, op0=ALU.mult, op1=ALU.add), [ssk], [ssk])
            for (xap, xkey, col0, ss, ssk, xn, xnk, pt, ptk) in st_:
                s.op("gpsimd", lambda e, ss=ss: e.tensor_tensor(out=ss[:, 2:3], in0=ss[:, 1:2], in1=mhalf[:, 0:1], op=ALU.pow), [ssk, "mhalf"], [ssk])
            for (xap, xkey, col0, ss, ssk, xn, xnk, pt, ptk) in st_:
                s.op("vector", lambda e, ss=ss, xn=xn, xap=xap: e.tensor_scalar(out=xn[:], in0=xap, scalar1=ss[:, 2:3], scalar2=None, op0=ALU.mult), [xkey, ssk], [xnk])
            return st_

        def prenorm_b(st_, hT, hkey, i, r):
            for (xap, xkey, col0, ss, ssk, xn, xnk, pt, ptk) in st_:
                for kc in range(8):
                    s.op("tensor", lambda e, kc=kc, pt=pt, xn=xn: e.transpose(out=pt[:, kc * 128:(kc + 1) * 128], in_=xn[:, kc * 128:(kc + 1) * 128], identity=identb[:]),
                         [xnk, "identb"], [ptk], inc=(kc == 7))
            for (xap, xkey, col0, ss, ssk, xn, xnk, pt, ptk) in st_:
                for kc in range(8):
                    A, S = mod_AS(i, kc, r)
                    s.op("vector", lambda e, kc=kc, A=A, S=S, pt=pt, col0=col0: e.tensor_scalar(out=hT[:, kc, col0:col0 + 128], in0=pt[:, kc * 128:(kc + 1) * 128],
                                                                                           scalar1=A, scalar2=S, op0=ALU.mult, op1=ALU.add),
                         [ptk, "modT"], [hkey])

        def prenorm_tiles(tiles, hT, hkey, i, r, xnb):
            for p0 in range(0, len(tiles), 2):
                prenorm_b(prenorm_a(tiles[p0:p0 + 2], xnb), hT, hkey, i, r)

        def prenorm_hooks(tiles, hT, hkey, i, r, xnb, steps):
            box = {}

            def a0():
                box[0] = prenorm_a(tiles[0:2], xnb)

            def b0():
                prenorm_b(box[0], hT, hkey, i, r)

            def a1():
                box[1] = prenorm_a(tiles[2:4], xnb)

            def b1():
                prenorm_b(box[1], hT, hkey, i, r)
            return dict(zip(steps, (a0, b0, a1, b1)))

        def ffn_block(xtiles, r, i, wu_b, wukey, wd, wdkey, hT, hkey, uT, wus, xnb, sab, tmpb, store=None, pre_done=False, hooks=None):
            nt = len(xtiles)
            ntok = nt * 128
            if not pre_done:
                prenorm_tiles([(xap, xkey, t * 128) for t, (xap, xkey) in enumerate(xtiles)], hT, hkey, i, r, xnb)
            wuv = wu_b.rearrange("(kc p) n -> p kc n", p=128)
            for jj in range(11):
                if hooks and jj in hooks:
                    hooks[jj]()
                sl = wus[jj % len(wus)]
                slk = ("wus", jj % len(wus))
                s.dma("sync", f"wu{jj % len(wus)}", [(sl[:], wuv[:, :, jj * 512:(jj + 1) * 512], [wukey], [slk], {})])
                for jl in range(2):
                    j = jj * 2 + jl
                    pa, pak = PF[(j % 2) * 2], f"PF{(j % 2) * 2}"
                    pb, pbk = PF[(j % 2) * 2 + 1], f"PF{(j % 2) * 2 + 1}"
                    for kc in range(8):
                        s.op("tensor", lambda e, kc=kc, sl=sl, jl=jl, pa=pa: e.matmul(pa[:, 0:ntok], lhsT=sl[:, kc, jl * 256:jl * 256 + 128], rhs=hT[:, kc, 0:ntok],
                                                                                       start=(kc == 0), stop=(kc == 7)),
                             [slk, hkey], [pak], inc=(kc == 7))
                    for kc in range(8):
                        s.op("tensor", lambda e, kc=kc, sl=sl, jl=jl, pb=pb: e.matmul(pb[:, 0:ntok], lhsT=sl[:, kc, jl * 256 + 128:jl * 256 + 256], rhs=hT[:, kc, 0:ntok],
                                                                                       start=(kc == 0), stop=(kc == 7)),
                             [slk, hkey], [pbk], inc=(kc == 7))
                    sa = sab[j % 2]
                    sak = ("sa", j % 2)
                    s.op("scalar", lambda e, sa=sa, pa=pa: e.activation(out=sa[:, 0:ntok], in_=pa[:, 0:ntok], func=AF.Silu), [pak], [sak])
                    s.op("vector", lambda e, sa=sa, pb=pb, j=j: e.tensor_tensor(out=uT[:, j, 0:ntok], in0=pb[:, 0:ntok], in1=sa[:, 0:ntok], op=ALU.mult),
                         [pbk, sak], [("uT", j)])
            for t, (xap, xkey) in enumerate(xtiles):
                p0 = (t % 3) * 2
                py = [PF[p0], PF[p0 + 1]]
                pyk = [f"PF{p0}", f"PF{p0 + 1}"]
                for hf in range(2):
                    for kc in range(NJ):
                        s.op("tensor", lambda e, kc=kc, hf=hf, t=t: e.matmul(py[hf][:, :], lhsT=uT[:, kc, t * 128:(t + 1) * 128], rhs=wd[:, kc, hf * 512:(hf + 1) * 512],
                                                                              start=(kc == 0), stop=(kc == NJ - 1)),
                             [("uT", kc), wdkey], [pyk[hf]], inc=(kc == NJ - 1))
                postnorm_residual(py, pyk, xap, xkey, tmpb)
                if store is not None:
                    s.dma("sync", "st", [(store[t], xap, [xkey], [("out", id(store), t)], {})])

        def postnorm_residual(py, pyk, xap, xkey, tmpb):
            ss, ssk = scol(4)
            jk = junk[0]
            jkk = ("junk", 0)
            rr["jk"] += 1
            s.op("scalar", lambda e: e.activation(out=jk[:, 0:512], in_=py[0][:, :], func=AF.Square, accum_out=ss[:, 0:1]), [pyk[0]], [ssk, jkk])
            s.op("scalar", lambda e: e.activation(out=jk[:, 512:1024], in_=py[1][:, :], func=AF.Square, accum_out=ss[:, 1:2]), [pyk[1]], [ssk, jkk])
            s.op("vector", lambda e: e.tensor_tensor(out=ss[:, 2:3], in0=ss[:, 0:1], in1=ss[:, 1:2], op=ALU.add), [ssk], [ssk])
            s.op("vector", lambda e: e.tensor_scalar(out=ss[:, 2:3], in0=ss[:, 2:3], scalar1=1.0 / D, scalar2=EPS, op0=ALU.mult, op1=ALU.add), [ssk], [ssk])
            s.op("gpsimd", lambda e: e.tensor_tensor(out=ss[:, 3:4], in0=ss[:, 2:3], in1=mhalf[:, 0:1], op=ALU.pow), [ssk, "mhalf"], [ssk])
            tmp = tmpb[rr["xn"] % 2]
            tk = ("tmpb", rr["xn"] % 2)
            rr["xn"] += 1
            for hf in range(2):
                s.op("vector", lambda e, hf=hf: e.scalar_tensor_tensor(out=tmp[:, hf * 512:(hf + 1) * 512], in0=py[hf][:, :], scalar=ss[:, 3:4],
                                                                        in1=gtile[:, hf * 512:(hf + 1) * 512], op0=ALU.mult, op1=ALU.mult),
                     [pyk[hf], ssk, "gtile"], [tk])
            s.op("gpsimd", lambda e: e.tensor_tensor(out=xap, in0=xap, in1=tmp[:], op=ALU.add), [xkey, tk], [xkey])

        def gates_block8(hT, hkey, ntok, chunk0, gt8):
            nck = ntok // 128
            zi, l1, cf, av = gt8
            gk = [("gt8", i) for i in range(4)]
            for gi, c0 in enumerate((0, 8)):
                pg, pgk = PF[4 + gi], f"PF{4 + gi}"
                for kc in range(8):
                    s.op("tensor", lambda e, kc=kc, c0=c0, pg=pg: e.matmul(pg[0:8, 0:ntok], lhsT=wing[:, kc, c0:c0 + 8], rhs=hT[:, kc, 0:ntok], start=(kc == 0), stop=(kc == 7)),
                         ["wing", hkey], [pgk], inc=(kc == 7))
            s.op("vector", lambda e: e.tensor_scalar(out=zi[:, 0:ntok], in0=PF[4][0:8, 0:ntok], scalar1=bg8[:, 0:1], scalar2=None, op0=ALU.add), ["PF4", "bg8"], [gk[0]])
            s.op("vector", lambda e: e.tensor_scalar(out=l1[:, 0:ntok], in0=PF[5][0:8, 0:ntok], scalar1=bg8[:, 1:2], scalar2=-1.0, op0=ALU.add, op1=ALU.mult), ["PF5", "bg8"], [gk[1]])
            s.op("scalar", lambda e: e.activation(out=l1[:, 0:ntok], in_=l1[:, 0:ntok], func=AF.Exp), [gk[1]], [gk[1]])
            s.op("scalar", lambda e: e.activation(out=l1[:, 0:ntok], in_=l1[:, 0:ntok], func=AF.Ln, bias=1.0), [gk[1]], [gk[1]])
            s.op("vector", lambda e: e.tensor_tensor_scan(out=cf[:, 0:ntok], data0=restart[:, 0:ntok], data1=l1[:, 0:ntok], initial=0.0, op0=ALU.mult, op1=ALU.add),
                 [gk[1], "restart"], [gk[2]])
            cf3 = cf[:, 0:ntok].rearrange("p (c t) -> p c t", t=128)
            l13 = l1[:, 0:ntok].rearrange("p (c t) -> p c t", t=128)
            av3 = av[:, 0:ntok].rearrange("p (c t) -> p c t", t=128)
            s.op("vector", lambda e: e.tensor_copy(out=btot8[:, chunk0:chunk0 + nck], in_=cf3[:, :, 127]), [gk[2]], ["btot8"])
            s.op("vector", lambda e: e.tensor_tensor(out=av3, in0=l13, in1=btot8[:, chunk0:chunk0 + nck].unsqueeze(2).to_broadcast([8, nck, 128]), op=ALU.add),
                 [gk[1], "btot8"], [gk[3]])
            s.op("vector", lambda e: e.tensor_scalar(out=av[:, 0:ntok], in0=av[:, 0:ntok], scalar1=cst[0:8, 909:910], scalar2=None, op0=ALU.mult), [gk[3], "cst"], [gk[3]])
            s.op("vector", lambda e: e.scalar_tensor_tensor(out=cf[:, 0:ntok], in0=cf[:, 0:ntok], scalar=cst[0:8, 908:909], in1=av[:, 0:ntok], op0=ALU.mult, op1=ALU.add),
                 [gk[2], gk[3], "cst"], [gk[2]])
            s.op("vector", lambda e: e.tensor_tensor(out=av[:, 0:ntok], in0=zi[:, 0:ntok], in1=cf[:, 0:ntok], op=ALU.add), [gk[0], gk[2]], [gk[3]])
            s.op("vector", lambda e: e.tensor_reduce(out=amax8[:, chunk0:chunk0 + nck], in_=av3, axis=AX.X, op=ALU.max), [gk[3]], ["amax8"])
            for c in range(nck):
                for qi, (src_, skey) in enumerate(((av, gk[3]), (cf, gk[2]))):
                    col = ((c * 2) + qi) * 8
                    s.op("tensor", lambda e, c=c, src_=src_, col=col: e.matmul(PF[3][:, col:col + 8], lhsT=src_[0:8, c * 128:(c + 1) * 128], rhs=cst[0:8, 0:8], start=True, stop=True),
                         [skey, "cst"], ["PF3"], inc=(c == nck - 1 and qi == 1))
            pv = PF[3][:, 0:nck * 16].rearrange("p (c q x) -> p c q x", q=2, x=8)
            s.op("vector", lambda e: e.tensor_copy(out=atm[:, chunk0:chunk0 + nck, :, :].rearrange("p c d h -> p c (d h)"), in_=pv[:, :, 0, :]), ["PF3"], ["atm"])
            s.op("scalar", lambda e: e.copy(out=btm[:, chunk0:chunk0 + nck, :, :].rearrange("p c d h -> p c (d h)"), in_=pv[:, :, 1, :]), ["PF3"], ["btm"])

        def gates_split():
            s.op("vector", lambda e: e.tensor_copy(out=amax[:, 0, :], in_=amax8[0:4, :]), ["amax8"], ["amax"])
            s.op("vector", lambda e: e.tensor_copy(out=btot[:, 0, :], in_=btot8[0:4, :]), ["btot8"], ["btot"])
            s.dma("sync", "gsp", [(amax[:, 1, :], amax8[4:8, :], ["amax8"], ["amax"], {}), (btot[:, 1, :], btot8[4:8, :], ["btot8"], ["btot"], {})])

        def gates_finish(nh, nck, order, atm, btm, amax, btot, m0, mout, Mrow, grow_, wk, el, gB, tmp4):
            for d in range(2):
                prev = m0[0:nh, d:d + 1]
                pk = "m0"
                for idx, c in enumerate(order[d]):
                    s.op("vector", lambda e, c=c, d=d, prev=prev: e.tensor_tensor(out=Mrow[0:nh, d, c:c + 1], in0=prev, in1=amax[0:nh, d, c:c + 1], op=ALU.max),
                         ["amax", "m0", "mout", "mcur", "mnext"], ["Mrow"])
                    s.op("vector", lambda e, c=c, d=d, prev=prev: e.tensor_tensor(out=grow_[0:nh, d, c:c + 1], in0=prev, in1=Mrow[0:nh, d, c:c + 1], op=ALU.subtract),
                         ["Mrow", "m0", "mout", "mcur", "mnext"], ["growg"])
                    s.op("vector", lambda e, c=c, d=d: e.tensor_tensor(out=tmp4[0:nh, d, c:c + 1], in0=Mrow[0:nh, d, c:c + 1], in1=btot[0:nh, d, c:c + 1], op=ALU.subtract),
                         ["Mrow", "btot"], ["mnext"])
                    prev = tmp4[0:nh, d, c:c + 1]
                    pk = "mnext"
                s.op("vector", lambda e, d=d, prev=prev: e.tensor_copy(out=mout[0:nh, d:d + 1], in_=prev), ["mnext"], ["mout" if mout is not mdump else "mdump"])
            s.op("scalar", lambda e: e.activation(out=grow_[0:nh, :, 0:nck], in_=grow_[0:nh, :, 0:nck], func=AF.Exp), ["growg"], ["growg"])
            for qi, (row, rk) in enumerate(((Mrow, "Mrow"), (grow_, "growg"))):
                ex, exk = tmp4, "tmp4x"
                exv = el
                R = wk
                s.op("vector", lambda e, row=row: e.tensor_tensor(out=Rexp[0:nh, :, 0:nck, :], in0=row[0:nh, :, 0:nck].unsqueeze(3).to_broadcast([nh, 2, nck, 4]),
                                                                  in1=cst[0:nh, 896:900].unsqueeze(1).unsqueeze(1).to_broadcast([nh, 2, nck, 4]), op=ALU.mult),
                     [rk, "cst"], ["Rexp"])
                for d in range(2):
                    s.op("tensor", lambda e, d=d, qi=qi: e.matmul(PF[3][:, (qi * 2 + d) * 64:(qi * 2 + d) * 64 + nck * 4], lhsT=cst[0:nh, 384:512],
                                                                    rhs=Rexp[0:nh, d, 0:nck, :].rearrange("p c h -> p (c h)"), start=True, stop=True),
                         ["Rexp", "cst"], ["PF3"], inc=True)
            Mb = PF[3][:, 0:128].rearrange("p (d c h) -> p c d h", d=2, c=16, h=4)
            gb = PF[3][:, 128:256].rearrange("p (d c h) -> p c d h", d=2, c=16, h=4)
            s.op("vector", lambda e: e.tensor_copy(out=gB[:, 0:nck, :, :], in_=gb[:, 0:nck, :, :]), ["PF3"], ["gB"])
            s.op("vector", lambda e: e.tensor_tensor(out=wk[:, 0:nck, :, :], in0=atm[:, 0:nck, :, :], in1=Mb[:, 0:nck, :, :], op=ALU.subtract), ["PF3", "atm"], ["wk"])
            s.op("vector", lambda e: e.tensor_tensor(out=el[:, 0:nck, :, :], in0=btm[:, 0:nck, :, :], in1=Mb[:, 0:nck, :, :], op=ALU.subtract), ["PF3", "btm"], ["el"])
            s.op("scalar", lambda e: e.activation(out=wk[:, 0:nck, :, :], in_=wk[:, 0:nck, :, :], func=AF.Exp), ["wk"], ["wk"])
            s.op("scalar", lambda e: e.activation(out=el[:, 0:nck, :, :], in_=el[:, 0:nck, :, :], func=AF.Exp), ["el"], ["el"])

        Rexp = T("Rexp", [4, 2, 16, 4], F32)
        restart = T("restart", [8, 512], F32)
        s.op("vector", lambda e: e.memset(restart[:], 1.0), [], ["restart"])
        s.op("vector", lambda e: e.memset(restart[:].rearrange("p (c t) -> p c t", t=128)[:, :, 0:1], 0.0), ["restart"], ["restart"])
        atm = T("atm", [128, 16, 2, 4], F32)
        btm = T("btm", [128, 16, 2, 4], F32)
        wk = T("wk", [128, 16, 2, 4], F32)
        el = T("el", [128, 16, 2, 4], F32)
        gB = T("gB", [128, 16, 2, 4], F32)
        amax = T("amax", [4, 2, 16], F32)
        btot = T("btot", [4, 2, 16], F32)
        Mrow = T("Mrow", [4, 2, 16], F32)
        growg = T("growg", [4, 2, 16], F32)
        tmp4 = T("tmp4", [4, 2, 16], F32)
        mzero = T("mzero", [4, 2], F32)
        bg8 = T("bg8", [8, 2], F32)
        amax8 = T("amax8", [8, 16], F32)
        btot8 = T("btot8", [8, 16], F32)
        s.dma("sync", "c4", [(bg8[:], bg8_d, [], ["bg8"], {})])
        s.op("vector", lambda e: e.memset(amax8[:], 0.0), [], ["amax8"])
        s.op("vector", lambda e: e.memset(btot8[:], 0.0), [], ["btot8"])
        mcur = T("mcur", [4, 2], F32)
        s.op("vector", lambda e: e.memset(mzero[:], 0.0), [], ["m0"])

        def scan_gen(d, chunks, nh, h0, kT, ktk, vx, vxk, qT, qtk, C, Ck, nS, nk, sc, outputs, out_cb=None):
            W = nh * 128
            pt = PT[d]
            ps, pn, pu = PF[d], PF[2 + d], PF[4 + d]
            kS, kN, kD, kU, kPk = ("ps", d), ("pn", d), ("pn", d), ("pu", d), ("ptK", d)
            pun, kUn = (pu, kU) if nh == 2 else (pn, kN)
            ktm, ktmk = sc["ktm"], ("ktm", d)
            vw, vwk = sc["vw"], ("vw", d)
            C3 = C.rearrange("p (h x) -> p h x", h=nh)
            for (ci, gi) in chunks:
                tsl = slice(ci * 128, (ci + 1) * 128)
                for h in range(nh):
                    s.op("tensor", lambda e, h=h: e.transpose(out=pt[:, h * 128:(h + 1) * 128], in_=kT[:, h, tsl], identity=identb[:]),
                         [ktk, "identb"], [kPk], inc=(h == nh - 1))
                    yield
                s.op("scalar", lambda e: e.copy(out=ktm[:, 0:W], in_=pt[:, 0:W]), [kPk], [ktmk])
                yield
                s.op("gpsimd", lambda e, ci=ci, gi=gi: e.tensor_tensor(out=vw[:, 0:nh, :], in0=vx[:, ci, 0:nh, :],
                                                                        in1=wk[:, gi, d, h0:h0 + nh].unsqueeze(2).to_broadcast([128, nh, 129]), op=ALU.mult),
                     [vxk, "wk"], [vwk])
                yield
                s.op("vector", lambda e, gi=gi: e.tensor_tensor(out=C3, in0=C3, in1=gB[:, gi, d, h0:h0 + nh].unsqueeze(2).to_broadcast([128, nh, 128]), op=ALU.mult),
                     [Ck, "gB"], [Ck])
                yield
                s.op("vector", lambda e, gi=gi: e.tensor_tensor(out=nS, in0=nS, in1=gB[:, gi, d, h0:h0 + nh], op=ALU.mult), [nk, "gB"], [nk])
                yield
                if outputs:
                    cgb, cgk = sc["cgb"], ("cgb", d)
                    ngb = sc["ngb"]
                    sm, smk = sc["sm"], ("sm", d)
                    s.op("scalar", lambda e: e.copy(out=cgb[:, 0:W], in_=C), [Ck], [cgk])
                    yield
                    s.op("scalar", lambda e: e.copy(out=ngb[:, 0:nh], in_=nS), [nk], [cgk])
                    yield
                    for h in range(nh):
                        s.op("tensor", lambda e, h=h: e.matmul(ps[:, h * 128:(h + 1) * 128], lhsT=kT[:, h, tsl], rhs=qT[:, h, tsl], start=True, stop=True),
                             [ktk, qtk], [kS], inc=(h == nh - 1))
                        yield
                    s.op("vector", lambda e: e.tensor_tensor(out=sm[:, 0:nh, :], in0=ps[:, 0:W].rearrange("p (h j) -> p h j", h=nh),
                                                             in1=maskb[:, d, :].unsqueeze(1).to_broadcast([128, nh, 128]), op=ALU.mult),
                         [kS, "maskb"], [smk])
                    yield
                for h in range(nh):
                    s.op("tensor", lambda e, h=h: e.matmul(pu[:, h * 128:(h + 1) * 128], lhsT=ktm[:, h * 128:(h + 1) * 128], rhs=vw[:, h, 0:128], start=True, stop=True),
                         [ktmk, vwk], [kU], inc=(h == nh - 1))
                    yield
                for h in range(nh):
                    s.op("tensor", lambda e, h=h: e.matmul(pun[:, 448 + h:449 + h], lhsT=ktm[:, h * 128:(h + 1) * 128], rhs=vw[:, h, 128:129], start=True, stop=True),
                         [ktmk, vwk], [kUn], inc=(h == nh - 1))
                    yield
                if outputs:
                    for h in range(nh):
                        s.op("tensor", lambda e, h=h: e.matmul(pn[:, h * 128:(h + 1) * 128], lhsT=qT[:, h, tsl], rhs=cgb[:, h * 128:(h + 1) * 128], start=True, stop=False),
                             [qtk, cgk], [kN], inc=False)
                        s.op("tensor", lambda e, h=h: e.matmul(pn[:, h * 128:(h + 1) * 128], lhsT=sm[:, h, :], rhs=vw[:, h, 0:128], start=False, stop=True),
                             [smk, vwk], [kN], inc=(h == nh - 1))
                        yield
                    for h in range(nh):
                        s.op("tensor", lambda e, h=h: e.matmul(pn[:, 448 + h:449 + h], lhsT=qT[:, h, tsl], rhs=ngb[:, h:h + 1], start=True, stop=False),
                             [qtk, cgk], [kD], inc=False)
                        s.op("tensor", lambda e, h=h: e.matmul(pn[:, 448 + h:449 + h], lhsT=sm[:, h, :], rhs=vw[:, h, 128:129], start=False, stop=True),
                             [smk, vwk], [kD], inc=(h == nh - 1))
                        yield
                    dn, dnk = scol(8)
                    s.op("scalar", lambda e, dn=dn: e.activation(out=dn[:, 0:nh], in_=pn[:, 448:448 + nh], func=AF.Abs), [kD], [dnk])
                    yield
                    s.op("vector", lambda e, gi=gi, dn=dn: e.tensor_tensor(out=dn[:, 0:nh], in0=dn[:, 0:nh], in1=el[:, gi, d, h0:h0 + nh], op=ALU.max), [dnk, "el"], [dnk])
                    yield
                    s.op("vector", lambda e, dn=dn: e.reciprocal(out=dn[:, 4:4 + nh], in_=dn[:, 0:nh]), [dnk], [dnk])
                    yield
                    yield from out_cb(d, ci, pn, kN, dn[:, 4:4 + nh], dnk)
                s.op("vector", lambda e: e.tensor_tensor(out=C, in0=C, in1=pu[:, 0:W], op=ALU.add), [Ck, kU], [Ck])
                yield
                s.op("vector", lambda e: e.tensor_tensor(out=nS, in0=nS, in1=pun[:, 448:448 + nh], op=ALU.add), [nk, kUn], [nk])
                yield

        spawn_list = []

        def run_interleaved(gens):
            gens = list(gens)
            while gens or spawn_list:
                gens.extend(spawn_list)
                del spawn_list[:]
                for g_ in list(gens):
                    try:
                        next(g_)
                    except StopIteration:
                        gens.remove(g_)

        winv = win_b.rearrange("(kc p) n -> p kc n", p=128)
        wsn = [0]

        def wload(wsl, col0, ncol):
            k = wsn[0] % len(wsl)
            wsn[0] += 1
            sl = wsl[k]
            s.dma("sync", f"wi{k}", [(sl[:, :, 0:ncol], winv[:, :, col0:col0 + ncol], ["win_b"], [("wsl", k)], {})])
            return sl, ("wsl", k)

        def proj_fm(wsl, col0, hT, hkey, ntok, pbank, pkey, coff=0):
            sl, slk = wsl
            for kc in range(8):
                s.op("tensor", lambda e, kc=kc: e.matmul(pbank[:, 0:ntok], lhsT=sl[:, kc, coff:coff + 128], rhs=hT[:, kc, 0:ntok], start=(kc == 0), stop=(kc == 7)),
                     [slk, hkey], [pkey], inc=(kc == 7))

        try:
          ckpt("P")
          with ExitStack() as sx:
              xc = T("xc", [128, 4, D], F32, sx)
              hTx = T("hTx", [128, 8, 512], BF16, sx)
              uT = T("uTx", [128, NJ, 512], BF16, sx)
              wd = T("wdx", [128, NJ, D], BF16, sx)
              wus = [T(f"wusx{i}", [128, 8, 512], BF16, sx) for i in range(2)]
              xnb = [T(f"xnbx{i}", [128, D], BF16, sx) for i in range(2)]
              sab = [T(f"sabx{i}", [128, 512], F32, sx) for i in range(2)]
              tmpb = [T(f"tmpbx{i}", [128, D], F32, sx) for i in range(2)]
              kTx = T("kTx", [128, 4, 512], BF16, sx)
              vxx = T("vxx", [128, 4, 4, 129], BF16, sx)
              gt8 = [T(f"gt8x_{i}", [8, 512], F32, sx) for i in range(4)]
              scx = [{"ktm": T(f"ktmx{i}", [128, 512], BF16, sx), "vw": T(f"vwx{i}", [128, 4, 129], BF16, sx)} for i in range(2)]
              ctxv = ctx_d.rearrange("(t p) d -> p t d", p=128)
              s.dma("sync", "xl", [(xc[:, t, :], ctxv[:, t, :], [], [("xc", t)], {}) for t in range(4)])
              s.dma("sync", "wd", [(wd[:, 0:11, :], w1d_b.rearrange("(kc p) n -> p kc n", p=128)[:, 0:11, :], ["w1d_b"], ["wdx"], {}),
                                   (wd[:, 11:22, :], w1d_b.rearrange("(kc p) n -> p kc n", p=128)[:, 11:22, :], ["w1d_b"], ["wdx"], {})])
              load_gate(0, 2)
              s.op("vector", lambda e: e.memset(vxx[:, :, :, 128:129], 1.0), [], ["vxx"])
              s.op("vector", lambda e: e.memset(Cst[:].rearrange("p a b c -> p (a b c)"), 0.0), [], ["Cst"])
              s.op("vector", lambda e: e.memset(nst[:].rearrange("p a b c -> p (a b c)"), 0.0), [], ["nst"])
              xt = [(xc[:, t, :], ("xc", t)) for t in range(4)]
              ffn_block(xt, 2, 0, w1u_b, "w1u_b", wd, "wdx", hTx, "hTx", uT, wus, xnb, sab, tmpb)
              ckpt("X1")
              prenorm_tiles([(xc[:, t, :], ("xc", t), t * 128) for t in range(4)], hTx, "hTx2", 1, 2, xnb)
              wsl = [T(f"wslx{i}", [128, 8, 512], BF16, sx) for i in range(2)]
              wk_sl = wload(wsl, 2048, 512)
              for h in range(4):
                  proj_fm(wk_sl, 0, hTx, "hTx2", 512, PF[h % 2], f"PF{h % 2}", coff=h * 128)
                  s.op("scalar", lambda e, h=h: e.activation(out=kTx[:, h, :], in_=PF[h % 2][:, :], func=AF.Copy, scale=HD ** -0.5), [f"PF{h % 2}"], ["kTx"])
              wv_sl, wvk = wload(wsl, 2560, 512)
              for t in range(4):
                  pb, pk = PF[2 + t % 2], f"PF{2 + t % 2}"
                  for kc in range(8):
                      s.op("tensor", lambda e, kc=kc, t=t, pb=pb: e.matmul(pb[:, :], lhsT=hTx[:, kc, t * 128:(t + 1) * 128], rhs=wv_sl[:, kc, :], start=(kc == 0), stop=(kc == 7)),
                           ["hTx2", wvk], [pk], inc=(kc == 7))
                  s.op("scalar", lambda e, t=t, pb=pb: e.copy(out=vxx[:, t, :, 0:128], in_=pb[:, :].rearrange("p (h x) -> p h x", h=4)), [pk], ["vxx"])
              ckpt("X2")
              gates_block8(hTx, "hTx2", 512, 0, gt8)
              ckpt("X3a")
              gates_split()
              ckpt("X3")
              for b in range(2):
                  av = lambda t, b=b: t[:, 2 * b:2 * b + 2, :, :]
                  rv = lambda t, b=b: t[:, :, 2 * b:2 * b + 2]
                  gates_finish(4, 2, [[0, 1], [1, 0]], av(atm), av(btm), rv(amax), rv(btot), mzero, mst[:, b, :], rv(Mrow), rv(growg), av(wk), av(el), av(gB), rv(tmp4))
                  s.barrier()
                  gens = []
                  for d in range(2):
                      chunks = [(2 * b + c, 2 * b + c) for c in ([0, 1] if d == 0 else [1, 0])]
                      gens.append(scan_gen(d, chunks, 4, 0, kTx, "kTx", vxx, "vxx", None, None, Cst[:, b, d, :], ("Cst", b, d), nst[:, b, d, :], ("nst", b, d), scx[d], False))
                  run_interleaved(gens)
              s.barrier()

          ckpt("X")
          for b in range(2):
              with ExitStack() as sb:
                  xs = T("xs", [128, NT, D], F32, sb)
                  xv = x_d[b].rearrange("(t p) d -> p t d", p=128)
                  for blk in range(NT // TB):
                      s.dma("sync", f"xl{blk}", [(xs[:, t, :], xv[:, t, :], [], [("xs", t)], {}) for t in range(blk * TB, (blk + 1) * TB)])
                  xtl = [(xs[:, t, :], ("xs", t)) for t in range(NT)]
                  with ExitStack() as sa_:
                      hT2 = [T(f"hTa{i}", [128, 8, 512], BF16, sa_) for i in range(2)]
                      uT = T("uTa", [128, NJ, 512], BF16, sa_)
                      wd = T("wda", [128, NJ, D], BF16, sa_)
                      wus = [T(f"wusa{i}", [128, 8, 512], BF16, sa_) for i in range(2)]
                      xnb = [T(f"xnba{i}", [128, D], BF16, sa_) for i in range(2)]
                      sab = [T(f"saba{i}", [128, 512], F32, sa_) for i in range(2)]
                      tmpb = [T(f"tmpba{i}", [128, D], F32, sa_) for i in range(2)]
                      wdv = w1d_b.rearrange("(kc p) n -> p kc n", p=128)
                      s.dma("sync", "wd", [(wd[:, 0:11, :], wdv[:, 0:11, :], ["w1d_b"], ["wda"], {}), (wd[:, 11:22, :], wdv[:, 11:22, :], ["w1d_b"], ["wda"], {})])
                      load_gate(0, b)
                      for blk in range(NT // TB):
                          hooks = None
                          if blk + 1 < NT // TB:
                              nt_ = [(xs[:, (blk + 1) * TB + t, :], ("xs", (blk + 1) * TB + t), t * 128) for t in range(TB)]
                              hooks = prenorm_hooks(nt_, hT2[(blk + 1) % 2], ("hTa", (blk + 1) % 2), 0, b, xnb, (2, 4, 6, 8))
                          ffn_block(xtl[blk * TB:(blk + 1) * TB], b, 0, w1u_b, "w1u_b", wd, "wda", hT2[blk % 2], ("hTa", blk % 2), uT, wus, xnb, sab, tmpb,
                                    pre_done=(blk > 0), hooks=hooks)
                      s.barrier()
                  ckpt("A")
                  if dbg and b == 0:
                      s.dma("sync", "dbg", [(dbg_out["dbg_x1"].rearrange("(t p) d -> p t d", p=128), xs[:], [("xs", t) for t in range(NT)], ["dbgx1"], {})])
                  with ExitStack() as sm_:
                      mixT = T("mixT", [128, 8, L], BF16, sm_)
                      for g in range(2):
                          hs = [2 * g, 2 * g + 1]
                          with ExitStack() as sg:
                              qT = T("qT", [128, 2, L], BF16, sg)
                              kT = T("kT", [128, 2, L], BF16, sg)
                              vx = T("vx", [128, NT, 2, 129], BF16, sg)
                              sgo = T("sgo", [128, NT, 256], BF16, sg)
                              s.op("vector", lambda e: e.memset(vx[:, :, :, 128:129], 1.0), [], ["vx"])
                              with ExitStack() as sB:
                                  hT2 = [T(f"hTb{i}", [128, 8, 512], BF16, sB) for i in range(2)]
                                  xnb = [T(f"xnbb{i}", [128, D], BF16, sB) for i in range(2)]
                                  wsl = [T(f"wslb{i}", [128, 8, 256], BF16, sB) for i in range(3)]
                                  cvt = [T(f"cvt{i}", [128, 512], F32, sB) for i in range(4)]
                                  if g == 0:
                                      gt8 = [T(f"gt8b_{i}", [8, 512], F32, sB) for i in range(4)]
                                  th = T("th", [128, 256], F32, sB)
                                  for blk in range(NT // TB):
                                      hT, hkey = hT2[blk % 2], ("hTb", blk % 2)
                                      tok0 = blk * 512
                                      if blk == 0:
                                          prenorm_tiles([(xs[:, blk * TB + t, :], ("xs", blk * TB + t), t * 128) for t in range(TB)], hT, hkey, 1, b, xnb)
                                      hk_ = {}
                                      if blk + 1 < NT // TB:
                                          nt_ = [(xs[:, (blk + 1) * TB + t, :], ("xs", (blk + 1) * TB + t), t * 128) for t in range(TB)]
                                          hk_ = prenorm_hooks(nt_, hT2[(blk + 1) % 2], ("hTb", (blk + 1) % 2), 1, b, xnb, (0, 1, 2, 3))
                                      if 0 in hk_:
                                          hk_[0]()
                                      wq = wload(wsl, 1536 + hs[0] * 128, 256)
                                      for hh in range(2):
                                          proj_fm(wq, 0, hT, hkey, 512, PF[hh], f"PF{hh}", coff=hh * 128)
                                          s.op("scalar", lambda e, hh=hh: e.copy(out=qT[:, hh, tok0:tok0 + 512], in_=PF[hh][:, :]), [f"PF{hh}"], ["qT"])
                                      if 1 in hk_:
                                          hk_[1]()
                                      wkk = wload(wsl, 2048 + hs[0] * 128, 256)
                                      for hh in range(2):
                                          proj_fm(wkk, 0, hT, hkey, 512, PF[hh], f"PF{hh}", coff=hh * 128)
                                          s.op("scalar", lambda e, hh=hh: e.activation(out=kT[:, hh, tok0:tok0 + 512], in_=PF[hh][:, :], func=AF.Copy, scale=HD ** -0.5), [f"PF{hh}"], ["kT"])
                                      if 2 in hk_:
                                          hk_[2]()
                                      wb = wload(wsl, hs[0] * 128, 256)
                                      wc = wload(wsl, 512 + hs[0] * 128, 256)
                                      wu_ = wload(wsl, 1024 + hs[0] * 128, 256)
                                      for jj in range(2):
                                          j = 2 * g + jj
                                          pb0, pb1, pb2 = PF[3 * jj], PF[3 * jj + 1], PF[3 * jj + 2]
                                          k0, k1, k2 = f"PF{3 * jj}", f"PF{3 * jj + 1}", f"PF{3 * jj + 2}"
                                          proj_fm(wc, 0, hT, hkey, 512, pb1, k1, coff=jj * 128)
                                          proj_fm(wu_, 0, hT, hkey, 512, pb2, k2, coff=jj * 128)
                                          proj_fm(wb, 0, hT, hkey, 512, pb0, k0, coff=jj * 128)
                                          z_, cz_ = cvt[2 * jj], cvt[2 * jj + 1]
                                          zk, czk = ("cvt", 2 * jj), ("cvt", 2 * jj + 1)
                                          s.op("scalar", lambda e, cz_=cz_, pb1=pb1: e.copy(out=cz_[:], in_=pb1[:, :]), [k1], [czk])
                                          s.op("vector", lambda e, z_=z_, cz_=cz_, pb2=pb2: e.tensor_tensor(out=z_[:], in0=pb2[:, :], in1=cz_[:], op=ALU.mult), [k2, czk], [zk])
                                          s.op("gpsimd", lambda e, j=j, z_=z_, cz_=cz_: e.tensor_scalar(out=cz_[:], in0=z_[:], scalar1=cwT[:, j * 3 + 1:j * 3 + 2], scalar2=None, op0=ALU.mult), [zk, "cwT"], [czk])
                                          z3 = z_[:].rearrange("p (r w) -> p r w", w=64)
                                          c3 = cz_[:].rearrange("p (r w) -> p r w", w=64)
                                          s.op("vector", lambda e, j=j, z3=z3, c3=c3: e.scalar_tensor_tensor(out=c3[:, :, 1:64], in0=z3[:, :, 0:63], scalar=cwT[:, j * 3:j * 3 + 1], in1=c3[:, :, 1:64], op0=ALU.mult, op1=ALU.add),
                                               [zk, czk, "cwT"], [czk])
                                          s.op("vector", lambda e, j=j, z3=z3, c3=c3: e.scalar_tensor_tensor(out=c3[:, :, 0:63], in0=z3[:, :, 1:64], scalar=cwT[:, j * 3 + 2:j * 3 + 3], in1=c3[:, :, 0:63], op0=ALU.mult, op1=ALU.add),
                                               [zk, czk, "cwT"], [czk])
                                          s.op("vector", lambda e, j=j, cz_=cz_, pb0=pb0: e.tensor_tensor(out=mixT[:, j, tok0:tok0 + 512], in0=pb0[:, :], in1=cz_[:], op=ALU.mult), [k0, czk], [("mixT", j)])
                                          if jj == 0 and 3 in hk_:
                                              hk_[3]()
                                      wv = wload(wsl, 2560 + hs[0] * 128, 256)
                                      wo = wload(wsl, 3072 + hs[0] * 128, 256)
                                      for t in range(TB):
                                          ci = blk * TB + t
                                          pb, pk = PF[t % 2], f"PF{t % 2}"
                                          for kc in range(8):
                                              s.op("tensor", lambda e, kc=kc, t=t, pb=pb: e.matmul(pb[:, 0:256], lhsT=hT[:, kc, t * 128:(t + 1) * 128], rhs=wv[0][:, kc, :], start=(kc == 0), stop=(kc == 7)),
                                                   [hkey, wv[1]], [pk], inc=(kc == 7))
                                          for kc in range(8):
                                              s.op("tensor", lambda e, kc=kc, t=t, pb=pb: e.matmul(pb[:, 256:512], lhsT=hT[:, kc, t * 128:(t + 1) * 128], rhs=wo[0][:, kc, :], start=(kc == 0), stop=(kc == 7)),
                                                   [hkey, wo[1]], [pk], inc=(kc == 7))
                                          s.op("scalar", lambda e, ci=ci, pb=pb: e.copy(out=vx[:, ci, :, 0:128], in_=pb[:, 0:256].rearrange("p (h x) -> p h x", h=2)), [pk], ["vx"])
                                          s.op("scalar", lambda e, pb=pb: e.activation(out=th[:], in_=pb[:, 256:512], func=AF.Tanh, scale=0.5), [pk], ["th"])
                                          s.op("vector", lambda e, ci=ci: e.scalar_tensor_tensor(out=sgo[:, ci, :], in0=th[:], scalar=1.0, in1=mhnb[:, hs[0] * 128:hs[0] * 128 + 256], op0=ALU.add, op1=ALU.mult),
                                               ["th", "mhnb"], ["sgo"])
                                      if g == 0:
                                          gates_block8(hT, hkey, 512, blk * TB, gt8)
                                  if g == 0:
                                      gates_split()
                                      gates_finish(4, NT, [list(range(NT)), list(range(NT - 1, -1, -1))], atm, btm, amax, btot, mst[:, b, :], mdump, Mrow, growg, wk, el, gB, tmp4)
                                  s.barrier()
                              ckpt("B")
                              with ExitStack() as sC:
                                  scd = [{"ktm": T(f"ktm{i}", [128, 256], BF16, sC), "vw": T(f"vw{i}", [128, 2, 129], BF16, sC),
                                          "cgb": T(f"cgb{i}", [128, 256], BF16, sC), "ngb": T(f"ngb{i}", [128, 2], BF16, sC),
                                          "sm": T(f"sm{i}", [128, 2, 128], BF16, sC)} for i in range(2)]
                                  hst = T("hst", [128, NT, 256], BF16, sC)
                                  Cw = T("Cw", [128, 2, 256], F32, sC)
                                  nw = T("nw", [128, 2, 2], F32, sC)
                                  hsum = [[T(f"hsum{i}_{p}", [128, 256], F32, sC) for p in range(2)] for i in range(2)]
                                  hsq = [T(f"hsq{i}", [128, 256], F32, sC) for i in range(2)]
                                  hn = [T(f"hn{i}", [128, 256], BF16, sC) for i in range(2)]
                                  for d in range(2):
                                      s.op("vector", lambda e, d=d: e.tensor_copy(out=Cw[:, d, :], in_=Cst[:, b, d, hs[0] * 128:hs[0] * 128 + 256]), [("Cst", b, d)], [("Cw", d)])
                                      s.op("vector", lambda e, d=d: e.tensor_copy(out=nw[:, d, :], in_=nst[:, b, d, hs[0]:hs[0] + 2]), [("nst", b, d)], [("nw", d)])
                                  first_half = {0: set(range(0, NT // 2)), 1: set(range(NT // 2, NT))}

                                  def out_cb(d, ci, pn, pnk, rden, rk):
                                      if ci in first_half[d]:
                                          s.op("vector", lambda e: e.tensor_tensor(out=hst[:, ci, :].rearrange("p (h x) -> p h x", h=2), in0=pn[:, 0:256].rearrange("p (h x) -> p h x", h=2),
                                                                                   in1=rden.unsqueeze(2).to_broadcast([128, 2, 128]), op=ALU.mult), [pnk, rk], [("hst", ci)])
                                          yield
                                          return
                                      par = ci % 2
                                      hsm, hk = hsum[d][par], ("hsum", d, par)
                                      s.op("vector", lambda e: e.tensor_tensor(out=hsm[:].rearrange("p (h x) -> p h x", h=2), in0=pn[:, 0:256].rearrange("p (h x) -> p h x", h=2),
                                                                               in1=rden.unsqueeze(2).to_broadcast([128, 2, 128]), op=ALU.mult), [pnk, rk], [hk])
                                      yield
                                      spawn_list.append(combine_rest(d, ci, hsm, hk))

                                  def combine_rest(d, ci, hsm, hk):
                                      sq, sqk = hsq[d], ("hsq", d)
                                      s.op("gpsimd", lambda e: e.tensor_tensor(out=hsm[:], in0=hsm[:], in1=hst[:, ci, :], op=ALU.add), [hk, ("hst", ci)], [hk])
                                      yield
                                      st_, stk = scol(8)
                                      s.op("gpsimd", lambda e: e.tensor_tensor(out=sq[:], in0=hsm[:], in1=hsm[:], op=ALU.mult), [hk], [sqk])
                                      yield
                                      s.op("vector", lambda e: e.tensor_reduce(out=st_[:, 0:2], in_=sq[:].rearrange("p (h x) -> p h x", h=2), axis=AX.X, op=ALU.add), [sqk], [stk])
                                      yield
                                      s.op("vector", lambda e: e.tensor_scalar(out=st_[:, 2:4], in0=st_[:, 0:2], scalar1=1.0 / HD, scalar2=EPS, op0=ALU.mult, op1=ALU.add), [stk], [stk])
                                      yield
                                      s.op("gpsimd", lambda e: e.tensor_tensor(out=st_[:, 4:6], in0=st_[:, 2:4], in1=mhalf[:, 0:2], op=ALU.pow), [stk, "mhalf"], [stk])
                                      yield
                                      s.op("vector", lambda e: e.tensor_tensor(out=hsm[:].rearrange("p (h x) -> p h x", h=2), in0=hsm[:].rearrange("p (h x) -> p h x", h=2),
                                                                               in1=st_[:, 4:6].unsqueeze(2).to_broadcast([128, 2, 128]), op=ALU.mult), [hk, stk], [hk])
                                      yield
                                      hnn, hnk = hn[d], ("hn", d)
                                      s.op("gpsimd", lambda e: e.tensor_tensor(out=hnn[:], in0=hsm[:], in1=sgo[:, ci, :], op=ALU.mult), [hk, "sgo"], [hnk])
                                      yield
                                      pt, ptk = PT[d], ("ptK", d)
                                      for hh in range(2):
                                          s.op("tensor", lambda e, hh=hh: e.transpose(out=pt[:, 512 + hh * 128:512 + (hh + 1) * 128], in_=hnn[:, hh * 128:(hh + 1) * 128], identity=identb[:]),
                                               [hnk, "identb"], [ptk], inc=(hh == 1))
                                      yield
                                      s.op("scalar", lambda e: e.copy(out=mixT[:, 4 + hs[0]:4 + hs[0] + 2, ci * 128:(ci + 1) * 128], in_=pt[:, 512:768].rearrange("p (h x) -> p h x", h=2)),
                                           [ptk], [("mixT", 4 + hs[0], ci), ("mixT", 5 + hs[0], ci)])
                                      yield

                                  gens = [scan_gen(0, [(c, c) for c in range(NT)], 2, hs[0], kT, "kT", vx, "vx", qT, "qT", Cw[:, 0, :], ("Cw", 0), nw[:, 0, :], ("nw", 0), scd[0], True, out_cb),
                                          scan_gen(1, [(c, c) for c in range(NT - 1, -1, -1)], 2, hs[0], kT, "kT", vx, "vx", qT, "qT", Cw[:, 1, :], ("Cw", 1), nw[:, 1, :], ("nw", 1), scd[1], True, out_cb)]
                                  run_interleaved(gens)
                                  for jx in (4 + hs[0], 5 + hs[0]):
                                      s.lastw[("mixT", jx)] = ("scalar", s.cnt["scalar"])
                                      s.reads[("mixT", jx)] = []
                                  s.barrier()
                      ckpt("C")
                      if dbg and b == 0:
                          s.dma("sync", "dbg", [(dbg_out["dbg_mix"], mixT[:].rearrange("p a b -> p (a b)"), [("mixT", j) for j in range(8)], ["dbgmix"], {})])
                      with ExitStack() as s2:
                          wo_t = T("wo_t", [128, 8, D], BF16, s2)
                          tmpb = [T(f"tmpbc{i}", [128, D], F32, s2) for i in range(2)]
                          s.dma("sync", "wo", [(wo_t[:], wout_b.rearrange("(kc p) n -> p kc n", p=128), ["wout_b"], ["wo_t"], {})])
                          load_gate(1, b)
                          for t in range(NT):
                              p0 = (t % 3) * 2
                              py = [PF[p0], PF[p0 + 1]]
                              pyk = [f"PF{p0}", f"PF{p0 + 1}"]
                              for hf in range(2):
                                  for kc in range(8):
                                      s.op("tensor", lambda e, kc=kc, hf=hf, t=t: e.matmul(py[hf][:, :], lhsT=mixT[:, kc, t * 128:(t + 1) * 128], rhs=wo_t[:, kc, hf * 512:(hf + 1) * 512],
                                                                                            start=(kc == 0), stop=(kc == 7)),
                                           [("mixT", kc), "wo_t"], [pyk[hf]], inc=(kc == 7))
                              postnorm_residual(py, pyk, xs[:, t, :], ("xs", t), tmpb)
                          s.barrier()
                  if dbg and b == 0:
                      s.dma("sync", "dbg", [(dbg_out["dbg_x2"].rearrange("(t p) d -> p t d", p=128), xs[:], [("xs", t) for t in range(NT)], ["dbgx2"], {})])
                  with ExitStack() as sd_:
                      hT2 = [T(f"hTd{i}", [128, 8, 512], BF16, sd_) for i in range(2)]
                      uT = T("uTd", [128, NJ, 512], BF16, sd_)
                      wd = T("wdd", [128, NJ, D], BF16, sd_)
                      wus = [T(f"wusd{i}", [128, 8, 512], BF16, sd_) for i in range(2)]
                      xnb = [T(f"xnbd{i}", [128, D], BF16, sd_) for i in range(2)]
                      sab = [T(f"sabd{i}", [128, 512], F32, sd_) for i in range(2)]
                      tmpb = [T(f"tmpbd{i}", [128, D], F32, sd_) for i in range(2)]
                      wdv = w2d_b.rearrange("(kc p) n -> p kc n", p=128)
                      s.dma("sync", "wd", [(wd[:, 0:11, :], wdv[:, 0:11, :], ["w2d_b"], ["wdd"], {}), (wd[:, 11:22, :], wdv[:, 11:22, :], ["w2d_b"], ["wdd"], {})])
                      load_gate(2, b)
                      ov = out_d[b].rearrange("(t p) d -> t p d", p=128)
                      for blk in range(NT // TB):
                          hooks = None
                          if blk + 1 < NT // TB:
                              nt_ = [(xs[:, (blk + 1) * TB + t, :], ("xs", (blk + 1) * TB + t), t * 128) for t in range(TB)]
                              hooks = prenorm_hooks(nt_, hT2[(blk + 1) % 2], ("hTd", (blk + 1) % 2), 2, b, xnb, (2, 4, 6, 8))
                          ffn_block(xtl[blk * TB:(blk + 1) * TB], b, 2, w2u_b, "w2u_b", wd, "wdd", hT2[blk % 2], ("hTd", blk % 2), uT, wus, xnb, sab, tmpb,
                                    store=[ov[blk * TB + t] for t in range(TB)], pre_done=(blk > 0), hooks=hooks)
                      s.barrier()
        except _Stop:
            pass
        s.barrier()
        print("ops", s.nops, "waits", s.nwaits, flush=True)
    return nc


_CACHE = {}


def _prep_inputs(x, c, ctx, c_ctx, w_mod, b_mod, norm_g, ffn1_up, ffn1_down, ffn2_up, ffn2_down,
                 w_in, b_gates, conv_w, mh_norm, w_out):
    f = lambda a: np.ascontiguousarray(np.asarray(a, dtype=np.float32))

    def perm_up(w):
        w = f(w)
        return np.ascontiguousarray(w.reshape(D, 2, NJ, 128).transpose(0, 2, 1, 3).reshape(D, 2 * FF))

    w_in = f(w_in[0])
    shared = {
        "w_mod": f(w_mod[0]),
        "b_mod3": np.ascontiguousarray(np.broadcast_to(f(b_mod[0])[None, :], (3, 9 * D))),
        "norm_g3": np.ascontiguousarray(np.broadcast_to(f(norm_g[0]).reshape(1, 6 * D), (3, 6 * D))),
        "w1u": perm_up(ffn1_up[0]), "w1d": f(ffn1_down[0]),
        "w2u": perm_up(ffn2_up[0]), "w2d": f(ffn2_down[0]),
        "w_in_main": np.ascontiguousarray(w_in[:, :3584]),
        "w_in_g": np.ascontiguousarray(w_in[:, 3584:3600]),
        "bgT": np.ascontiguousarray(f(b_gates[0]).reshape(4, 4).T),
        "bg8": np.ascontiguousarray(f(b_gates[0]).reshape(2, 8).T),
        "conv_wT": np.ascontiguousarray(f(conv_w[0]).T.reshape(4, 128, 3).transpose(1, 0, 2).reshape(128, 12)),
        "mhn_b": np.ascontiguousarray(np.broadcast_to(f(mh_norm[0])[None, :], (128, 512))),
        "w_out": f(w_out[0]),
        "consts": _consts_np(),
    }
    x = np.asarray(x, dtype=np.float32)
    ctx = np.asarray(ctx, dtype=np.float32)
    c = np.asarray(c, dtype=np.float32)
    c_ctx = np.asarray(c_ctx, dtype=np.float32)
    in_maps = []
    for i in range(NCORES):
        m = dict(shared)
        m["x"] = np.ascontiguousarray(x[2 * i:2 * i + 2])
        m["ctx"] = np.ascontiguousarray(ctx[2 * i:2 * i + 2].reshape(2 * CL, D))
        m["cc"] = np.ascontiguousarray(np.stack([c[2 * i], c[2 * i + 1], c_ctx], 0))
        in_maps.append(m)
    return in_maps


def kernel(**inputs):
    in_maps = _prep_inputs(**inputs)
    if "nc" not in _CACHE:
        _CACHE["nc"] = build_program()
    nc = _CACHE["nc"]
    res = run_bass_kernel_spmd(nc, in_maps, core_ids=list(range(NCORES)))
    out = np.concatenate([np.asarray(r["out"]) for r in res.results], axis=0)
    return out.astype(np.float32)
```
